# Optimizing a Trainium2 kernel written in Bass

```python
import jax, jax.numpy as jnp
from jax import lax
import numpy as np

D_MODEL = 1024
BATCH = 4
SEQ = 8192
DEPTH = 4

N_HEADS = 16
N_KV_HEADS = 2
HEAD_DIM = 64
GROUP = N_HEADS // N_KV_HEADS
ROT_DIM = HEAD_DIM // 4
ROPE_THETA = 500000.0
WINDOW = 128
BLOCK = 128
CONV_CH = D_MODEL // 2
CONV_WIDTH = 31
D_FF = -(-(8 * D_MODEL) // (3 * 256)) * 256
EPS = 1e-6
Q_W = N_HEADS * HEAD_DIM
KV_W = N_KV_HEADS * HEAD_DIM
IN_W = Q_W + 2 * KV_W + 2 * CONV_CH + 2 * D_MODEL
SPLITS = tuple(int(s) for s in np.cumsum([Q_W, KV_W, KV_W, CONV_CH, CONV_CH, D_MODEL])[:])

kernel_name = "hybrid_swa_sink_conformer_gated"


def rmsnorm(x, g):
    xf = x.astype(jnp.float32)
    y = xf * lax.rsqrt(jnp.mean(xf * xf, axis=-1, keepdims=True) + EPS)
    return (y * g.astype(jnp.float32)).astype(x.dtype)


def layernorm(x, g, b):
    xf = x.astype(jnp.float32)
    mu = jnp.mean(xf, axis=-1, keepdims=True)
    var = jnp.mean(jnp.square(xf - mu), axis=-1, keepdims=True)
    y = (xf - mu) * lax.rsqrt(var + EPS)
    return (y * g.astype(jnp.float32) + b.astype(jnp.float32)).astype(x.dtype)


def rope_tables(seq):
    inv_freq = ROPE_THETA ** (-jnp.arange(0, ROT_DIM, 2, dtype=jnp.float32) / ROT_DIM)
    ang = jnp.arange(seq, dtype=jnp.float32)[:, None] * inv_freq[None, :]
    return jnp.cos(ang), jnp.sin(ang)


def partial_rope(x, cos, sin):
    half = ROT_DIM // 2
    c = cos[None, :, None, :].astype(x.dtype)
    s = sin[None, :, None, :].astype(x.dtype)
    x1, x2, xp = x[..., :half], x[..., half:ROT_DIM], x[..., ROT_DIM:]
    return jnp.concatenate([x1 * c - x2 * s, x2 * c + x1 * s, xp], axis=-1)


def sliding_window_attention(q, k, v, sinks):
    B, T = q.shape[0], q.shape[1]
    nb = T // BLOCK
    qb = q.reshape(B, nb, BLOCK, N_KV_HEADS, GROUP, HEAD_DIM)

    def band(t):
        tp = jnp.pad(t, ((0, 0), (BLOCK, 0), (0, 0), (0, 0)))
        tp = tp.reshape(B, nb + 1, BLOCK, N_KV_HEADS, HEAD_DIM)
        return jnp.concatenate([tp[:, :-1], tp[:, 1:]], axis=2)

    kb, vb = band(k), band(v)
    scale = HEAD_DIM ** -0.5
    s = jnp.einsum('bnqkgd,bnskd->bnkgqs', qb, kb,
                   preferred_element_type=jnp.float32) * scale
    qi = jnp.arange(BLOCK)[:, None]
    sj = jnp.arange(2 * BLOCK)[None, :]
    rel = qi + BLOCK - sj
    kpos = jnp.arange(nb)[:, None, None] * BLOCK - BLOCK + sj
    mask = (rel >= 0) & (rel < WINDOW) & (kpos >= 0)
    s = jnp.where(mask[None, :, None, None], s, -jnp.inf)
    sink = sinks.astype(jnp.float32).reshape(N_KV_HEADS, GROUP)[None, None, :, :, None, None]
    m = jnp.maximum(jnp.max(s, axis=-1, keepdims=True), sink)
    p = jnp.exp(s - m)
    p = p / (jnp.sum(p, axis=-1, keepdims=True) + jnp.exp(sink - m))
    o = jnp.einsum('bnkgqs,bnskd->bnqkgd', p.astype(v.dtype), vb)
    return o.reshape(B, T, N_HEADS * HEAD_DIM)


def conformer_conv(u, ug, w_dw, b_dw, ln_g, ln_b, w_pw):
    a = u * jax.nn.sigmoid(ug)
    y = lax.conv_general_dilated(
        a, w_dw[:, None, :].astype(a.dtype), window_strides=(1,),
        padding=[(CONV_WIDTH - 1, 0)],
        dimension_numbers=('NWC', 'WIO', 'NWC'),
        feature_group_count=CONV_CH) + b_dw.astype(a.dtype)
    y = layernorm(y, ln_g, ln_b)
    y = jax.nn.silu(y)
    return jnp.einsum('btc,cd->btd', y, w_pw)


def setup_inputs(seed: int = 0) -> dict:
    key = jax.random.key(seed)
    ks = jax.random.split(key, 16)
    f32 = jnp.float32

    def nrm(k, shape, scale):
        return jax.random.normal(k, shape, f32) * scale

    return {
        "x": nrm(ks[0], (BATCH, SEQ, D_MODEL), 1.0),
        "norm_mix": 1.0 + nrm(ks[1], (DEPTH, D_MODEL), 0.02),
        "w_in": nrm(ks[2], (DEPTH, D_MODEL, IN_W), D_MODEL ** -0.5),
        "q_norm": 1.0 + nrm(ks[3], (DEPTH, HEAD_DIM), 0.02),
        "k_norm": 1.0 + nrm(ks[4], (DEPTH, HEAD_DIM), 0.02),
        "sinks": nrm(ks[5], (DEPTH, N_HEADS), 0.5),
        "conv_w": nrm(ks[6], (DEPTH, CONV_WIDTH, CONV_CH), CONV_WIDTH ** -0.5),
        "conv_b": nrm(ks[7], (DEPTH, CONV_CH), 0.02),
        "conv_ln_g": 1.0 + nrm(ks[8], (DEPTH, CONV_CH), 0.02),
        "conv_ln_b": nrm(ks[9], (DEPTH, CONV_CH), 0.02),
        "w_conv_out": nrm(ks[10], (DEPTH, CONV_CH, D_MODEL), CONV_CH ** -0.5),
        "w_out": nrm(ks[11], (DEPTH, D_MODEL, D_MODEL), D_MODEL ** -0.5),
        "norm_ffn": 1.0 + nrm(ks[12], (DEPTH, D_MODEL), 0.02),
        "w_gate_up": nrm(ks[13], (DEPTH, D_MODEL, 2 * D_FF), D_MODEL ** -0.5),
        "w_down": nrm(ks[14], (DEPTH, D_FF, D_MODEL), D_FF ** -0.5),
    }


def reference(x, norm_mix, w_in, q_norm, k_norm, sinks, conv_w, conv_b, conv_ln_g,
              conv_ln_b, w_conv_out, w_out, norm_ffn, w_gate_up, w_down):
    B, T = x.shape[0], x.shape[1]
    cos, sin = rope_tables(T)
    for l in range(DEPTH):
        h = rmsnorm(x, norm_mix[l])
        proj = jnp.einsum('btd,de->bte', h, w_in[l])
        q, k, v, u, ug, ga, gb = jnp.split(proj, SPLITS, axis=-1)
        q = q.reshape(B, T, N_HEADS, HEAD_DIM)
        k = k.reshape(B, T, N_KV_HEADS, HEAD_DIM)
        v = v.reshape(B, T, N_KV_HEADS, HEAD_DIM)
        q = partial_rope(rmsnorm(q, q_norm[l]), cos, sin)
        k = partial_rope(rmsnorm(k, k_norm[l]), cos, sin)
        a_out = sliding_window_attention(q, k, v, sinks[l])
        c_out = conformer_conv(u, ug, conv_w[l], conv_b[l], conv_ln_g[l],
                               conv_ln_b[l], w_conv_out[l])
        merged = jax.nn.sigmoid(ga) * a_out + jax.nn.sigmoid(gb) * c_out
        x = x + jnp.einsum('btd,de->bte', merged, w_out[l])
        h2 = rmsnorm(x, norm_ffn[l])
        gu = jnp.einsum('btd,df->btf', h2, w_gate_up[l])
        g, up = jnp.split(gu, 2, axis=-1)
        x = x + jnp.einsum('btf,fd->btd', jax.nn.silu(g) * up, w_down[l])
    return x
```

```python
import numpy as np
from contextlib import ExitStack
import concourse.bass as bass
import concourse.mybir as mybir
from concourse.bass_utils import run_bass_kernel_spmd

F32 = mybir.dt.float32
BF16 = mybir.dt.bfloat16
I32 = mybir.dt.int32
AF = mybir.ActivationFunctionType
ALU = mybir.AluOpType

D = 1024
KC = 8
TS = 512
HALO = 512
NH, NKV, HD = 16, 2, 64
CONV_CH, CONV_W = 512, 31
DFF = 2816
NF = DFF // 128
EPS = 1e-6
ROPE_THETA = 500000.0
NSLOT = 4
NTF, NTB = 6, 4
NEG = -30000.0

Q0, K0, V0, U0, UG0, GA0, GB0 = 0, 1024, 1152, 1280, 1792, 2304, 3328


def _win_chunk_cols():
    ch = []
    ch.append(np.concatenate([np.arange(K0, K0 + 64), np.arange(K0, K0 + 64)]))
    ch.append(np.concatenate([np.arange(K0 + 64, K0 + 128), np.arange(K0 + 64, K0 + 128)]))
    ch.append(np.arange(V0, V0 + 128))
    for c in range(4):
        ch.append(np.arange(U0 + c * 128, U0 + (c + 1) * 128))
        ch.append(np.arange(UG0 + c * 128, UG0 + (c + 1) * 128))
    ch.append(None)
    for c in range(8):
        ch.append(np.arange(Q0 + c * 128, Q0 + (c + 1) * 128))
    for c in range(8):
        ch.append(np.arange(GA0 + c * 128, GA0 + (c + 1) * 128))
    for c in range(8):
        ch.append(np.arange(GB0 + c * 128, GB0 + (c + 1) * 128))
    assert len(ch) == 36
    return ch


CH_KD, CH_V, CH_U, CH_PADU, CH_Q, CH_GA, CH_GB = 0, 2, 3, 11, 12, 20, 28


def par_off(L):
    o = {}
    n = 0
    for name, sz in (("g1", L * 8), ("g2", L * 8), ("gq", L), ("gqp", L), ("gk", L), ("gkp", L),
                     ("sink", L * 8), ("cw", L * 4 * 31), ("cb", L * 4), ("lg", L * 4), ("lb", L * 4),
                     ("invf", 1), ("rm", 1), ("onem", 1), ("mhalf", 1), ("eps", 1), ("eps64", 1)):
        o[name] = n
        n += sz
    return o, n


class Sched:
    def __init__(self, nc, es):
        self.nc = nc
        self.es = es
        self.E = {"pe": nc.tensor, "act": nc.scalar, "dve": nc.vector, "pool": nc.gpsimd, "sp": nc.sync}
        self.sem = {}
        self.cnt = {}
        for e in ("pe", "act", "dve", "pool"):
            self.sem[e] = es.enter_context(nc.semaphore("s_" + e))
            self.cnt[e] = 0
        self.seen = {e: {} for e in self.E}
        self.W = {}
        self.R = {}
        self.nwait = 0
        self.dry = False

    def dma_sem(self, name):
        if name not in self.sem:
            self.sem[name] = self.es.enter_context(self.nc.semaphore("d_" + name))
            self.cnt[name] = 0
        return name

    @staticmethod
    def _cells(r):
        if isinstance(r, tuple) and r and r[0] == "sb":
            return [("c", i) for i in range(r[1] // 512, (r[2] + 511) // 512)]
        return [r]

    def _emit(self, eng, fn, reads, writes, evkey, inc, signal=True):
        if self.dry:
            return None
        raw = {}
        oth = {}
        rc = []
        wc = []
        for r in reads:
            for c in self._cells(r):
                rc.append(c)
                for k, v in self.W.get(c, {}).items():
                    if raw.get(k, 0) < v:
                        raw[k] = v
        for w in writes:
            for c in self._cells(w):
                wc.append(c)
                for dct in (self.W, self.R):
                    for k, v in dct.get(c, {}).items():
                        if oth.get(k, 0) < v:
                            oth[k] = v
        need = dict(raw)
        for k, v in oth.items():
            if k == eng and eng == "pe":
                continue
            if need.get(k, 0) < v:
                need[k] = v
        if eng == "pe":
            need.pop("pe", None)
        seen = self.seen[eng]
        for k, v in need.items():
            if seen.get(k, 0) >= v:
                continue
            self.E[eng].wait_ge(self.sem[k], v)
            seen[k] = v
            self.nwait += 1
        inst = fn(self.E[eng])
        ev = self.cnt[evkey] + inc
        if signal:
            inst.then_inc(self.sem[evkey], inc)
            self.cnt[evkey] = ev
        for c in rc:
            d = self.R.setdefault(c, {})
            if d.get(evkey, 0) < ev:
                d[evkey] = ev
        for c in wc:
            d = self.W.setdefault(c, {})
            if d.get(evkey, 0) < ev:
                d[evkey] = ev
        return inst

    def op(self, eng, fn, reads=(), writes=(), signal=True):
        return self._emit(eng, fn, reads, writes, eng, 1, signal)

    def dma(self, fn, reads, writes, sem, eng="sp"):
        if self.dry:
            return None
        self.dma_sem(sem)
        return self._emit(eng, fn, reads, writes, sem, 16, True)

    def final_wait(self, sems):
        for k in sems:
            if k in self.cnt and self.cnt[k] > 0:
                self.E["sp"].wait_ge(self.sem[k], self.cnt[k])


class Buf:
    def __init__(self, arena, off, dtype, shape):
        esz = 2 if dtype == BF16 else 4
        n = int(np.prod(shape))
        self.off = off
        self.nbytes = n * esz
        v = arena[:, off // 2:(off + n * esz) // 2]
        if dtype != BF16:
            v = v.bitcast(dtype)
        if len(shape) == 2:
            v = v.rearrange("p (a b) -> p a b", b=shape[1])
        elif len(shape) == 3:
            v = v.rearrange("p (a b c) -> p a b c", b=shape[1], c=shape[2])
        self.ap = v
        self.cb = self.nbytes // shape[0] if len(shape) > 1 else self.nbytes

    def res(self, i=None, n=1):
        if i is None:
            return ("sb", self.off, self.off + self.nbytes)
        return ("sb", self.off + i * self.cb, self.off + (i + n) * self.cb)


def build_program(P, NT, L):
    NTOK = NT * TS
    NB = NT * 4
    NOUT = NTOK - HALO
    PO, NPAR = par_off(L)
    nc = bass.Bass("TRN2", target_bir_lowering=False)

    def din(name, shape, dt=F32):
        return nc.dram_tensor(name, shape, dt, kind="ExternalInput").ap()

    xT_d = din("xT", [P, 128, 8, NTOK])
    wimg = {
        "in": din("w_in_img", [L, 9, 128, 4096]),
        "co": din("w_co_img", [L, 1, 128, 4096]),
        "o": din("w_o_img", [L, 2, 128, 4096]),
        "gu": din("w_gu_img", [L, 11, 128, 4096]),
        "d": din("w_d_img", [L, 8, 128, 2816]),
    }
    par_d = din("params", [128, NPAR])
    cmat_d = din("cmat", [128, 4 * 128])
    pos_d = din("pos", [P, 128, NTOK])
    kbias_d = din("kbias", [P, 128, NB + 1])
    valid_d = din("valid", [P, 128, 512])
    yT_d = nc.dram_tensor("yT", [P, 128, 8, NOUT], F32, kind="ExternalOutput").ap()
    wscr = {k: nc.dram_tensor("scr_" + k, list(v.shape), BF16, kind="Internal").ap() for k, v in wimg.items()}
    wscr["cd"] = nc.dram_tensor("scr_cd", [L, 4, 128, 31 * 128], BF16, kind="Internal").ap()
    rope_scr = nc.dram_tensor("rope_scr", [P, NT, 128, 1024], F32, kind="Internal").ap()
    NPIECE = {"in": 9, "co": 1, "o": 2, "gu": 11, "d": 8, "cd": 4}
    PCOLS = {"in": 4096, "co": 4096, "o": 4096, "gu": 4096, "d": 2816, "cd": 31 * 128}

    with ExitStack() as es:
        sizes = [
            ("xT", 8 * NTOK * 4), ("wring", NSLOT * 8192),
            ("hT", 8192), ("A8", 8192), ("F16", 16384), ("yF", 8192), ("GA8", 8192), ("aT", 4 * 542 * 2 + 32), ("zb", 4096),
            ("kT", 2 * 640 * 2), ("vb", 5 * 128 * 2), ("Eb", 8192), ("tmpF", NTF * 2048), ("tmpB", NTB * 1024),
            ("rope", 4096), ("par", ((NPAR * 4 + 63) // 64) * 64), ("cmb", 1024), ("sinke", max(64, L * 8 * 4)),
            ("kbias", ((NB + 1) * 4 + 63) // 64 * 64), ("valid", 2048),
        ]
        offs = {}
        o = 0
        for n_, s_ in sizes:
            offs[n_] = o
            o += (s_ + 511) // 512 * 512
        total = o
        arena_t = es.enter_context(nc.sbuf_tensor("arena", [128, total // 2], BF16))
        arena = arena_t[:, :]
        xT = Buf(arena, offs["xT"], F32, [8, NTOK])
        wring = Buf(arena, offs["wring"], BF16, [NSLOT, 4096])
        hT = Buf(arena, offs["hT"], BF16, [8, 512])
        A8 = Buf(arena, offs["A8"], BF16, [8, 512])
        ma = Buf(arena, offs["F16"], F32, [8, 512])
        yF = Buf(arena, offs["yF"], F32, [4, 512])
        hff = Buf(arena, offs["F16"], BF16, [24, 512])
        assert offs["yF"] == offs["F16"] + 16384
        GA8 = Buf(arena, offs["GA8"], BF16, [8, 512])
        aT = Buf(arena, offs["aT"], BF16, [4, 542])
        zb = Buf(arena, offs["zb"], BF16, [4, 512])
        kT = Buf(arena, offs["kT"], BF16, [2, 640])
        vb = Buf(arena, offs["vb"], BF16, [5, 128])
        Eb = Buf(arena, offs["Eb"], BF16, [4, 1024])
        tmpF = Buf(arena, offs["tmpF"], F32, [NTF, 512])
        tmpFi = Buf(arena, offs["tmpF"], I32, [NTF, 512])
        tmpB = Buf(arena, offs["tmpB"], BF16, [NTB, 512])
        rope = Buf(arena, offs["rope"], F32, [2, 512])
        par = Buf(arena, offs["par"], F32, [NPAR])
        cmb = Buf(arena, offs["cmb"], BF16, [4, 128])
        sinke = Buf(arena, offs["sinke"], F32, [L * 8])
        kbias = Buf(arena, offs["kbias"], F32, [NB + 1])
        valid = Buf(arena, offs["valid"], F32, [512])
        stgB = Buf(arena, offs["hT"], BF16, [2, 4096])

        psum = [es.enter_context(nc.psum_tensor("ps%d" % i, [128, 512], F32)) for i in range(8)]
        s = Sched(nc, es)
        st = {"bank": 0, "tf": 0, "tb": 0}

        reserved = set()

        def bank():
            b = st["bank"]
            while b in reserved:
                b = (b + 1) % 8
            st["bank"] = (b + 1) % 8
            return b

        def PR(b):
            return ("ps", b)

        def tF():
            i = st["tf"]
            st["tf"] = (i + 1) % NTF
            return i

        def tB():
            i = st["tb"]
            st["tb"] = (i + 1) % NTB
            return i

        def pcol(name, idx=0):
            c = PO[name] + idx
            return par.ap[:, c:c + 1]

        ones_b = cmb.ap[:, 0, :]
        bd_b = cmb.ap[:, 1, :]
        pm_b = cmb.ap[:, 2, :]
        id_b = cmb.ap[:, 3, :]
        CMR = cmb.res()
        PARR = par.res()

        pst = {"seq": [], "next": 0, "issued": 0, "rel": set()}

        def get_piece(kind, l, i):
            key = (kind, l, i)
            if s.dry:
                pst["seq"].append(key)
                return wring.ap[:, 0, :], wring.res(0), len(pst["seq"]) - 1
            seq = pst["seq"]
            j = pst["next"]
            assert seq[j] == key, (seq[j], key)
            pst["next"] = j + 1
            while pst["issued"] < len(seq) and pst["issued"] <= j + NSLOT - 1:
                q = pst["issued"]
                if q - NSLOT >= 0 and (q - NSLOT) not in pst["rel"]:
                    break
                kq, lq, iq = seq[q]
                sl = q % NSLOT
                nco = PCOLS[kq]
                s.dma(lambda e, sl=sl, nco=nco, kq=kq, lq=lq, iq=iq: e.dma_start(out=wring.ap[:, sl, 0:nco],
                                                                                   in_=wscr[kq][lq, iq, :, :]),
                      [("scr", kq, lq, iq)], [wring.res(sl)], "wslot%d" % sl)
                pst["issued"] = q + 1
            assert pst["issued"] > j, ("ring overflow", key)
            sl = j % NSLOT
            return wring.ap[:, sl, :], wring.res(sl), j

        def rel_piece(j):
            if not s.dry:
                pst["rel"].add(j)

        def mm(out, lhsT, rhs, start, stop, reads, writes, signal, **kw):
            s.op("pe", lambda e: e.matmul(out, lhsT, rhs, start=start, stop=stop, **kw), reads, writes, signal=signal)

        def emit_casts(l):
            for kind in ("in", "co", "o", "gu", "d"):
                npc = NPIECE[kind]
                sem = "cv_%s_%d" % (kind, l)
                src = wimg[kind][l].rearrange("a p c -> (a p) c")
                dst = wscr[kind][l].rearrange("a p c -> (a p) c")
                s.dma(lambda e, src=src, dst=dst: e.dma_start(out=dst, in_=src, max_dma_last_dim=4096), [],
                      [("scr", kind, l, i) for i in range(npc)], sem, eng="pool")

        def prologue():
            s.dma(lambda e: e.dma_start(out=par.ap, in_=par_d[:, :]), [], [PARR], "par")
            i0 = tF()
            tF()
            cm_stage = tmpF.ap[:, i0:i0 + 2, :].rearrange("p a b -> p (a b)")[:, 0:512]
            s.dma(lambda e: e.dma_start(out=cm_stage, in_=cmat_d[:, :]), [], [tmpF.res(i0, 2)], "cm")
            s.op("dve", lambda e: e.tensor_copy(out=cmb.ap.rearrange("p a b -> p (a b)"), in_=cm_stage),
                 [tmpF.res(i0, 2)], [CMR])
            s.op("act", lambda e: e.activation(out=sinke.ap, in_=par.ap[:, PO["sink"]:PO["sink"] + L * 8], func=AF.Exp),
                 [PARR], [sinke.res()])
            emit_casts(0)
            for l in range(L):
                for cc in range(4):
                    sl = (l * 4 + cc) % 2
                    wo = PO["cw"] + (l * 4 + cc) * 31
                    dst3 = stgB.ap[:, sl, 0:31 * 128].rearrange("p (a b) -> p a b", b=128)
                    s.op("dve", lambda e, dst3=dst3, wo=wo: e.tensor_tensor(
                        out=dst3, in0=id_b.unsqueeze(1).broadcast_to([128, 31, 128]),
                        in1=par.ap[:, wo:wo + 31].unsqueeze(2).broadcast_to([128, 31, 128]), op=ALU.mult),
                        [CMR, PARR], [stgB.res(sl)])
                    s.dma(lambda e, sl=sl, l=l, cc=cc: e.dma_start(out=wscr["cd"][l, cc, :, :], in_=stgB.ap[:, sl, 0:31 * 128]),
                          [stgB.res(sl)], [("scr", "cd", l, cc)], "cdst%d" % sl)
            rope_gen(0, 0)

        TWO_PI = 6.283185
        rope_done = set()

        def rope_gen(p, t):
            if (p, t) in rope_done or t >= NT or p >= P:
                return
            rope_done.add((p, t))
            iu, ir, im = tF(), tF(), tF()
            U = tmpF.ap[:, iu, :]
            Rr = tmpF.ap[:, ir, :]
            Mm = tmpF.ap[:, im, :]
            Ii = tmpFi.ap[:, im, :]
            rt = rope.ap
            s.dma(lambda e, U=U, p=p, t=t: e.dma_start(out=U, in_=pos_d[p, :, t * TS:(t + 1) * TS]), [],
                  [tmpF.res(iu)], "pos%d" % iu)
            s.op("dve", lambda e, U=U: e.tensor_scalar(out=U, in0=U, scalar1=pcol("invf"), scalar2=None,
                                                         op0=ALU.mult), [tmpF.res(iu), PARR], [tmpF.res(iu)])
            for which in (0, 1):
                sh = 0.25 if which == 0 else 0.0
                s.op("dve", lambda e, U=U, Rr=Rr, sh=sh: e.tensor_scalar(out=Rr, in0=U, scalar1=sh, scalar2=None,
                                                                          op0=ALU.add), [tmpF.res(iu)], [tmpF.res(ir)])
                s.op("dve", lambda e, Ii=Ii, Rr=Rr: e.tensor_copy(out=Ii, in_=Rr), [tmpF.res(ir)], [tmpF.res(im)])
                s.op("dve", lambda e, Ii=Ii, Mm=Mm: e.tensor_copy(out=Mm, in_=Ii), [tmpF.res(im)], [tmpF.res(im)])
                s.op("dve", lambda e, Rr=Rr, Mm=Mm: e.tensor_tensor(out=Rr, in0=Rr, in1=Mm, op=ALU.subtract),
                     [tmpF.res(ir), tmpF.res(im)], [tmpF.res(ir)])
                s.op("dve", lambda e, Rr=Rr, Mm=Mm: e.tensor_scalar(out=Mm, in0=Rr, scalar1=0.5, scalar2=None,
                                                                     op0=ALU.is_gt), [tmpF.res(ir)], [tmpF.res(im)])
                s.op("dve", lambda e, Rr=Rr, Mm=Mm: e.tensor_tensor(out=Rr, in0=Rr, in1=Mm, op=ALU.subtract),
                     [tmpF.res(ir), tmpF.res(im)], [tmpF.res(ir)])
                s.op("dve", lambda e, Rr=Rr, Mm=Mm: e.tensor_scalar(out=Mm, in0=Rr, scalar1=-0.5, scalar2=None,
                                                                     op0=ALU.is_lt), [tmpF.res(ir)], [tmpF.res(im)])
                s.op("dve", lambda e, Rr=Rr, Mm=Mm: e.tensor_tensor(out=Rr, in0=Rr, in1=Mm, op=ALU.add),
                     [tmpF.res(ir), tmpF.res(im)], [tmpF.res(ir)])
                s.op("act", lambda e, Rr=Rr: e.activation(out=Rr, in_=Rr, func=AF.Sin, scale=TWO_PI),
                     [tmpF.res(ir)], [tmpF.res(ir)])
                if which == 0:
                    s.op("dve", lambda e, Rr=Rr: e.tensor_scalar(out=rt[:, 0, :], in0=Rr, scalar1=pcol("rm"),
                                                                  scalar2=pcol("onem"), op0=ALU.mult, op1=ALU.add),
                         [tmpF.res(ir), PARR], [rope.res(0)])
                else:
                    s.op("dve", lambda e, Rr=Rr: e.tensor_scalar(out=rt[:, 1, :], in0=Rr, scalar1=pcol("rm"),
                                                                  scalar2=None, op0=ALU.mult),
                         [tmpF.res(ir), PARR], [rope.res(1)])
            s.dma(lambda e, p=p, t=t: e.dma_start(out=rope_scr[p, t, :, :], in_=rope.ap.rearrange("p a b -> p (a b)")),
                  [rope.res()], [("ropescr", p, t)], "ropest")

        def tile_cols(l, t):
            if t == 0:
                kb0 = min(l, 4)
                qb0 = min(l + 1, 4)
            else:
                kb0 = qb0 = 0
            return kb0, qb0

        def pow_rs(ir, n, expo=-0.5):
            rs = tmpF.ap[:, ir, 0:n]
            s.op("dve", lambda e: e.reciprocal(out=rs, in_=rs), [tmpF.res(ir)], [tmpF.res(ir)])

        def rms_a(l, t, cs):
            cols = slice(t * TS + cs.start, (t + 1) * TS)
            for c in range(8):
                s.op("act", lambda e, c=c: e.activation(out=A8.ap[:, c, cs], in_=xT.ap[:, c, cols], func=AF.Square),
                     [xT.res(c)], [A8.res(c)])

        def rms_b(l, t, gname, cs, use_valid):
            cols = slice(t * TS + cs.start, (t + 1) * TS)
            n = 512 - cs.start
            b = bank()
            for c in range(8):
                mm(psum[b][:, 0:n], ones_b, A8.ap[:, c, cs], c == 0, c == 7, [CMR, A8.res(c)], [PR(b)], c == 7)
            ir = tF()
            rs = tmpF.ap[:, ir, 0:n]
            s.op("act", lambda e: e.activation(out=rs, in_=psum[b][:, 0:n], func=AF.Sqrt, scale=1.0 / D, bias=pcol("eps")),
                 [PR(b), PARR], [tmpF.res(ir)])
            pow_rs(ir, n)
            if use_valid:
                s.op("dve", lambda e: e.tensor_tensor(out=rs, in0=rs, in1=valid.ap[:, cs], op=ALU.mult),
                     [tmpF.res(ir), valid.res()], [tmpF.res(ir)])
            for c in range(8):
                s.op("dve", lambda e, c=c: e.scalar_tensor_tensor(out=hT.ap[:, c, cs], in0=xT.ap[:, c, cols],
                                                                  scalar=pcol(gname, l * 8 + c), in1=rs,
                                                                  op0=ALU.mult, op1=ALU.mult),
                     [xT.res(c), PARR, tmpF.res(ir)], [hT.res(c)])

        def proj_chunk(wap, wres, pos, cs, src=None, kcn=8, wstride=512):
            src = hT if src is None else src
            n = 512 - cs.start
            b = bank()
            for kc in range(kcn):
                lhsT = wap[:, kc * wstride + pos * 128: kc * wstride + (pos + 1) * 128]
                mm(psum[b][:, 0:n], lhsT, src.ap[:, kc, cs], kc == 0, kc == kcn - 1, [wres, src.res(kc)], [PR(b)],
                   kc == kcn - 1)
            return b

        def qk_P(bA, cs, slot):
            n = 512 - cs.start
            A = psum[bA][:, 0:n]
            if slot < 2:
                sq_ap, sq_res = tmpB.ap[:, slot * 2, 0:n], tmpB.res(slot * 2)
                raw_ap, raw_res = tmpB.ap[:, slot * 2 + 1, 0:n], tmpB.res(slot * 2 + 1)
            else:
                sq_ap, sq_res = zb.ap[:, 0, 0:n], zb.res(0)
                raw_ap, raw_res = zb.ap[:, 1, 0:n], zb.res(1)
            s.op("act", lambda e: e.activation(out=sq_ap, in_=A, func=AF.Square), [PR(bA)], [sq_res])
            s.op("act", lambda e: e.activation(out=raw_ap, in_=A, func=AF.Copy), [PR(bA)], [raw_res])
            reserved.add(bA)
            return (bA, sq_ap, sq_res, raw_ap, raw_res)

        def qk_H(stt, gname, gpname, l, cs, out_ap, out_res):
            bA, sq_ap, sq_res, raw_ap, raw_res = stt
            n = 512 - cs.start
            A = psum[bA][:, 0:n]
            bB = bank()
            mm(psum[bB][:, 0:n], bd_b, sq_ap, True, True, [CMR, sq_res], [PR(bB)], True)
            bC = bank()
            mm(psum[bC][:, 0:n], pm_b, raw_ap, True, True, [CMR, raw_res], [PR(bC)], True)
            ir = tF()
            rs = tmpF.ap[:, ir, 0:n]
            s.op("act", lambda e: e.activation(out=rs, in_=psum[bB][:, 0:n], func=AF.Sqrt, scale=1.0, bias=pcol("eps64")),
                 [PR(bB), PARR], [tmpF.res(ir)])
            pow_rs(ir, n)
            i1, i2 = tF(), tF()
            t1 = tmpF.ap[:, i1, 0:n]
            t2 = tmpF.ap[:, i2, 0:n]
            s.op("dve", lambda e: e.scalar_tensor_tensor(out=t1, in0=A, scalar=pcol(gname, l), in1=rope.ap[:, 0, cs],
                                                         op0=ALU.mult, op1=ALU.mult),
                 [PR(bA), PARR, rope.res(0)], [tmpF.res(i1)])
            reserved.discard(bA)
            s.op("dve", lambda e: e.scalar_tensor_tensor(out=t2, in0=psum[bC][:, 0:n], scalar=pcol(gpname, l),
                                                         in1=rope.ap[:, 1, cs], op0=ALU.mult, op1=ALU.mult),
                 [PR(bC), PARR, rope.res(1)], [tmpF.res(i2)])
            s.op("pool", lambda e: e.tensor_tensor(out=t1, in0=t1, in1=t2, op=ALU.add),
                 [tmpF.res(i1), tmpF.res(i2)], [tmpF.res(i1)])
            s.op("pool", lambda e: e.tensor_tensor(out=out_ap, in0=t1, in1=rs, op=ALU.mult),
                 [tmpF.res(i1), tmpF.res(ir)], [out_res])

        cur_in = {"j": None}

        def in_chunk(l, j):
            if j % 4 == 0 or cur_in["j"] is None or cur_in["pc"] != j // 4:
                cur_in["wap"], cur_in["wres"], cur_in["h"] = get_piece("in", l, j // 4)
                cur_in["pc"] = j // 4
            cur_in["j"] = j
            return cur_in["wap"], cur_in["wres"], j % 4

        def in_done(j):
            if j % 4 == 3 or j == CH_PADU - 1:
                rel_piece(cur_in["h"])
                cur_in["j"] = None

        def mixer_tile(p, l, t):
            kb0, qb0 = tile_cols(l, t)
            KS = slice(kb0 * 128, 512)
            CS = slice(qb0 * 128, 512)
            nk = 512 - KS.start
            nq = 512 - CS.start
            cols = slice(t * TS + CS.start, (t + 1) * TS)
            s.dma(lambda e: e.dma_start(out=rope.ap.rearrange("p a b -> p (a b)"), in_=rope_scr[p, t, :, :]),
                  [("ropescr", p, t)], [rope.res()], "ropeld")
            if t == 0:
                s.op("pool", lambda e: e.memset(kT.ap[:, :, 0:128], 0.0), [], [kT.res()])
                s.op("pool", lambda e: e.memset(vb.ap[:, 0, :], 0.0), [], [vb.res()])
                s.op("pool", lambda e: e.memset(aT.ap[:, :, 0:30], 0.0), [], [aT.res()])
            else:
                s.op("pool", lambda e: e.tensor_copy(out=kT.ap[:, :, 0:128], in_=kT.ap[:, :, 512:640]), [kT.res()], [kT.res()])
                s.op("pool", lambda e: e.tensor_copy(out=vb.ap[:, 0, :], in_=vb.ap[:, 4, :]), [vb.res()], [vb.res()])
                s.op("pool", lambda e: e.tensor_copy(out=aT.ap[:, :, 0:30], in_=aT.ap[:, :, 512:542]), [aT.res()], [aT.res()])
            stk = []
            for kh in range(2):
                wap, wres, pos = in_chunk(l, CH_KD + kh)
                bA = proj_chunk(wap, wres, pos, KS)
                in_done(CH_KD + kh)
                stk.append(qk_P(bA, KS, kh))
            wap, wres, pos = in_chunk(l, CH_V)
            bV = bank()
            for blk in range(kb0, 4):
                for kc in range(8):
                    mm(psum[bV][:, blk * 128:(blk + 1) * 128], hT.ap[:, kc, blk * 128:(blk + 1) * 128],
                       wap[:, kc * 512 + pos * 128: kc * 512 + (pos + 1) * 128], kc == 0, kc == 7,
                       [wres, hT.res(kc)], [PR(bV)], kc == 7)
            in_done(CH_V)
            s.op("act", lambda e: e.activation(out=vb.ap[:, 1 + kb0:5, :].rearrange("p a b -> p (a b)"), in_=psum[bV][:, KS],
                                               func=AF.Copy), [PR(bV)], [vb.res()])
            for kh in range(2):
                qk_H(stk[kh], "gk", "gkp", l, KS, kT.ap[:, kh, 128 + KS.start:640], kT.res(kh))
            for cc in range(4):
                j = CH_U + 2 * cc
                wap, wres, pos = in_chunk(l, j)
                bU = proj_chunk(wap, wres, pos, KS)
                in_done(j)
                wap, wres, pos = in_chunk(l, j + 1)
                bG = proj_chunk(wap, wres, pos, KS)
                in_done(j + 1)
                if cur_in["j"] is None or cc == 0:
                    filler(1)
                it = tF()
                s.op("act", lambda e, bG=bG, it=it: e.activation(out=tmpF.ap[:, it, 0:nk], in_=psum[bG][:, 0:nk], func=AF.Tanh,
                                                               scale=0.5), [PR(bG)], [tmpF.res(it)])
                s.op("dve", lambda e, it=it, cc=cc, bU=bU: e.scalar_tensor_tensor(
                    out=aT.ap[:, cc, 30 + KS.start:542], in0=tmpF.ap[:, it, 0:nk], scalar=1.0, in1=psum[bU][:, 0:nk],
                    op0=ALU.add, op1=ALU.mult), [tmpF.res(it), PR(bU)], [aT.res(cc)])
            if nq == 0:
                flush_pending()
                return
            conv_st = {}

            def conv_half(cc, half):
                if half == 0:
                    wap, wres, h = get_piece("cd", l, cc)
                    b = bank()
                    reserved.add(b)
                    conv_st[cc] = (wap, wres, h, b)
                wap, wres, h, b = conv_st[cc]
                rng_ = range(0, 16) if half == 0 else range(16, 31)
                for jt in rng_:
                    mm(psum[b][:, 0:nq], wap[:, jt * 128:(jt + 1) * 128], aT.ap[:, cc, jt + CS.start:jt + 512], jt == 0, jt == 30,
                       [wres, aT.res(cc)], [PR(b)], jt == 15 or jt == 30)
                if half == 0:
                    return
                rel_piece(h)
                reserved.discard(b)
                cbo = PO["cb"] + l * 4 + cc
                iy, iq = cc, 4 + cc
                s.op("act", lambda e: e.activation(out=yF.ap[:, cc, CS], in_=psum[b][:, 0:nq], func=AF.Identity,
                                                   scale=0.5, bias=par.ap[:, cbo:cbo + 1]), [PR(b), PARR], [yF.res(cc)])
                s.op("act", lambda e: e.activation(out=zb.ap[:, cc, 0:nq], in_=psum[b][:, 0:nq], func=AF.Identity,
                                                   scale=0.5, bias=par.ap[:, cbo:cbo + 1]), [PR(b), PARR], [zb.res(cc)])
                s.op("act", lambda e: e.activation(out=tmpB.ap[:, cc, 0:nq], in_=psum[b][:, 0:nq], func=AF.Square,
                                                   scale=0.5, bias=par.ap[:, cbo:cbo + 1]), [PR(b), PARR], [tmpB.res(cc)])

            def ln_stats_and_chain():
                bS1 = bank()
                bS2 = bank()
                reserved.update((bS1, bS2))
                for cc in range(4):
                    mm(psum[bS1][:, 0:nq], ones_b, zb.ap[:, cc, 0:nq], cc == 0, cc == 3, [CMR, zb.res(cc)], [PR(bS1)], cc == 3)
                for cc in range(4):
                    mm(psum[bS2][:, 0:nq], ones_b, tmpB.ap[:, cc, 0:nq], cc == 0, cc == 3, [CMR, tmpB.res(cc)], [PR(bS2)], cc == 3)
                imean, ivar = tF(), tF()
                mean = tmpF.ap[:, imean, 0:nq]
                var = tmpF.ap[:, ivar, 0:nq]
                s.op("dve", lambda e: e.tensor_scalar(out=mean, in0=psum[bS1][:, 0:nq], scalar1=1.0 / CONV_CH, scalar2=None, op0=ALU.mult),
                     [PR(bS1)], [tmpF.res(imean)])
                s.op("dve", lambda e: e.tensor_tensor(out=var, in0=mean, in1=mean, op=ALU.mult),
                     [tmpF.res(imean)], [tmpF.res(ivar)])
                s.op("dve", lambda e: e.scalar_tensor_tensor(out=var, in0=psum[bS2][:, 0:nq], scalar=1.0 / CONV_CH, in1=var,
                                                             op0=ALU.mult, op1=ALU.subtract),
                     [PR(bS2), tmpF.res(ivar)], [tmpF.res(ivar)])
                reserved.discard(bS1)
                reserved.discard(bS2)
                s.op("dve", lambda e: e.tensor_scalar(out=var, in0=var, scalar1=0.0, scalar2=EPS, op0=ALU.max, op1=ALU.add),
                     [tmpF.res(ivar)], [tmpF.res(ivar)])
                s.op("act", lambda e: e.activation(out=var, in_=var, func=AF.Sqrt), [tmpF.res(ivar)], [tmpF.res(ivar)])
                pow_rs(ivar, nq)
                for cc in range(4):
                    s.op("dve", lambda e, cc=cc: e.tensor_tensor(out=yF.ap[:, cc, CS], in0=yF.ap[:, cc, CS], in1=mean, op=ALU.subtract),
                         [yF.res(cc), tmpF.res(imean)], [yF.res(cc)])
                    s.op("dve", lambda e, cc=cc: e.tensor_tensor(out=yF.ap[:, cc, CS], in0=yF.ap[:, cc, CS], in1=var, op=ALU.mult),
                         [yF.res(cc), tmpF.res(ivar)], [yF.res(cc)])

            def q_P(c):
                j = CH_Q + c
                wap, wres, pos = in_chunk(l, j)
                bA = proj_chunk(wap, wres, pos, CS)
                in_done(j)
                return qk_P(bA, CS, c % 3)

            stq = [None] * 8
            stq[0] = q_P(0)
            stq[1] = q_P(1)
            for c in range(8):
                if c + 2 < 8:
                    stq[c + 2] = q_P(c + 2)
                if c == 1 or c == 5:
                    filler(2)
                qk_H(stq[c], "gq", "gqp", l, CS, A8.ap[:, c, CS], A8.res(c))
            flush_pending()
            its = [(qb, kh) for qb in range(qb0, 4) for kh in range(2)]

            def att_qk(i):
                qb, kh = its[i]
                qc = slice(qb * 128, (qb + 1) * 128)
                buf = i % 2
                for kbi in range(2):
                    kcs = slice((qb + kbi) * 128, (qb + kbi + 1) * 128)
                    kcol = 4 * t + qb + kbi
                    ei = buf * 2 + kbi
                    bb2 = [bank(), bank()]
                    for ci in range(4):
                        c = kh * 4 + ci
                        for parh in range(2):
                            pr = slice(parh * 64, (parh + 1) * 64)
                            mm(psum[bb2[parh]][:, ci * 128:(ci + 1) * 128], kT.ap[pr, kh, kcs], A8.ap[pr, c, qc], True, True,
                               [kT.res(kh), A8.res(c)], [PR(bb2[parh])], ci == 3)
                    for parh in range(2):
                        b = bb2[parh]
                        s.op("act", lambda e, b=b, ei=ei, parh=parh, kcol=kcol: e.activation(
                            out=Eb.ap[:, ei, parh * 512:(parh + 1) * 512], in_=psum[b][:, :], func=AF.Exp, scale=8.0,
                            bias=kbias.ap[:, kcol:kcol + 1]), [PR(b), kbias.res()], [Eb.res(ei)])
                    ev = Eb.ap[:, ei, :].rearrange("p (a b) -> p a b", b=128)
                    if kbi == 1:
                        s.op("pool", lambda e, ev=ev: e.affine_select(out=ev, in_=ev, pattern=[[0, 8], [1, 128]],
                                                                     compare_op=ALU.is_ge, fill=0.0, base=0,
                                                                     channel_multiplier=-1), [Eb.res(ei)], [Eb.res(ei)])
                    else:
                        s.op("pool", lambda e, ev=ev: e.affine_select(out=ev, in_=ev, pattern=[[0, 8], [-1, 128]],
                                                                     compare_op=ALU.is_gt, fill=0.0, base=0,
                                                                     channel_multiplier=1), [Eb.res(ei)], [Eb.res(ei)])

            def att_pv(i):
                qb, kh = its[i]
                qc = slice(qb * 128, (qb + 1) * 128)
                buf = i % 2
                bO = bank()
                bD = bank()
                for (bb, is_den) in ((bO, False), (bD, True)):
                    for parh in range(2):
                        for kbi in range(2):
                            ei = buf * 2 + kbi
                            lhsT = ones_b[:, 0:64] if is_den else vb.ap[:, qb + kbi, kh * 64:(kh + 1) * 64]
                            rd = [CMR] if is_den else [vb.res()]
                            mm(psum[bb][parh * 64:(parh + 1) * 64, :], lhsT, Eb.ap[:, ei, parh * 512:(parh + 1) * 512],
                               kbi == 0, kbi == 1, rd + [Eb.res(ei)], [PR(bb)], (parh == 1 and kbi == 1),
                               tile_position=(0, parh * 64))
                idn = tF()
                dn = tmpF.ap[:, idn, :]
                so = l * 8 + kh * 4
                s.op("dve", lambda e: e.tensor_tensor(
                    out=dn.rearrange("p (a b) -> p a b", b=128), in0=psum[bD][:, :].rearrange("p (a b) -> p a b", b=128),
                    in1=sinke.ap[:, so:so + 4].unsqueeze(2).broadcast_to([128, 4, 128]),
                    op=ALU.add), [PR(bD), sinke.res()], [tmpF.res(idn)])
                pow_rs(idn, 512, -1.0)
                s.op("dve", lambda e: e.tensor_tensor(
                    out=ma.ap[:, kh * 4:(kh + 1) * 4, qc], in0=psum[bO][:, :].rearrange("p (a b) -> p a b", b=128),
                    in1=dn.rearrange("p (a b) -> p a b", b=128), op=ALU.mult),
                    [PR(bO), tmpF.res(idn)], [ma.res(kh * 4, 4)])

            def ga_chunk(c):
                j = CH_GA + c
                wap, wres, pos = in_chunk(l, j)
                b = proj_chunk(wap, wres, pos, CS)
                in_done(j)
                s.op("act", lambda e: e.activation(out=GA8.ap[:, c, CS], in_=psum[b][:, 0:nq], func=AF.Tanh, scale=0.5),
                     [PR(b)], [GA8.res(c)])

            conv_todo = [(cc, h) for cc in range(4) for h in range(2)]
            ga_next = 0
            att_qk(0)
            for i in range(len(its)):
                if i + 1 < len(its):
                    att_qk(i + 1)
                att_pv(i)
                if conv_todo:
                    conv_half(*conv_todo.pop(0))
                if ga_next < 8:
                    ga_chunk(ga_next)
                    ga_next += 1
            while conv_todo:
                conv_half(*conv_todo.pop(0))
            while ga_next < 8:
                ga_chunk(ga_next)
                ga_next += 1
            for c in range(8):
                s.op("dve", lambda e, c=c: e.scalar_tensor_tensor(out=ma.ap[:, c, CS], in0=GA8.ap[:, c, CS], scalar=1.0,
                                                                  in1=ma.ap[:, c, CS], op0=ALU.add, op1=ALU.mult),
                     [ma.res(c), GA8.res(c)], [ma.res(c)])
            ln_stats_and_chain()
            for c in range(8):
                j = CH_GB + c
                wap, wres, pos = in_chunk(l, j)
                bG = proj_chunk(wap, wres, pos, CS)
                in_done(j)
                s.op("act", lambda e, bG=bG, c=c: e.activation(out=GA8.ap[:, c, CS], in_=psum[bG][:, 0:nq], func=AF.Tanh, scale=0.5),
                     [PR(bG)], [GA8.res(c)])
            for cc in range(4):
                lgo = PO["lg"] + l * 4 + cc
                lbo = PO["lb"] + l * 4 + cc
                s.op("act", lambda e, cc=cc, lgo=lgo, lbo=lbo: e.activation(
                    out=zb.ap[:, cc, CS], in_=yF.ap[:, cc, CS], func=AF.Silu, scale=par.ap[:, lgo:lgo + 1],
                    bias=par.ap[:, lbo:lbo + 1]), [yF.res(cc), PARR], [zb.res(cc)])
            cwap, cwres, ch = get_piece("co", l, 0)
            for c in range(8):
                bC = bank()
                for cc in range(4):
                    mm(psum[bC][:, 0:nq], cwap[:, cc * 1024 + c * 128: cc * 1024 + (c + 1) * 128], zb.ap[:, cc, CS], cc == 0, cc == 3,
                       [cwres, zb.res(cc)], [PR(bC)], cc == 3)
                ip = tF()
                s.op("dve", lambda e, ip=ip, bC=bC, c=c: e.scalar_tensor_tensor(
                    out=tmpF.ap[:, ip, 0:nq], in0=GA8.ap[:, c, CS], scalar=1.0, in1=psum[bC][:, 0:nq], op0=ALU.add, op1=ALU.mult),
                    [GA8.res(c), PR(bC)], [tmpF.res(ip)])
                s.op("pool", lambda e, ip=ip, c=c: e.tensor_tensor(
                    out=A8.ap[:, c, CS], in0=tmpF.ap[:, ip, 0:nq], in1=ma.ap[:, c, CS], op=ALU.add),
                    [tmpF.res(ip), ma.res(c)], [A8.res(c)])
            rel_piece(ch)
            for c in range(8):
                if c % 4 == 0:
                    wap, wres, h = get_piece("o", l, c // 4)
                b = proj_chunk(wap, wres, c % 4, CS, src=A8)
                if c % 4 == 3:
                    rel_piece(h)
                s.op("dve", lambda e, b=b, c=c: e.scalar_tensor_tensor(
                    out=xT.ap[:, c, cols], in0=psum[b][:, 0:nq], scalar=0.5, in1=xT.ap[:, c, cols], op0=ALU.mult, op1=ALU.add),
                    [PR(b), xT.res(c)], [xT.res(c)])

        pending = []

        def filler(n=1):
            for _ in range(n):
                if pending:
                    pending.pop(0)()

        def flush_pending():
            while pending:
                pending.pop(0)()

        def ffn_tile(p, l, t, nxt):
            kb0, qb0 = tile_cols(l, t)
            CS = slice(qb0 * 128, 512)
            nq = 512 - CS.start
            cols = slice(t * TS + CS.start, (t + 1) * TS)
            flush_pending()
            if nq > 0:
                rms_a(l, t, CS)
                rms_b(l, t, "g2", CS, False)
                for f in range(NF):
                    if f % 2 == 0:
                        wap, wres, h = get_piece("gu", l, f // 2)
                    bG = proj_chunk(wap, wres, (2 * f) % 4, CS)
                    bU = proj_chunk(wap, wres, (2 * f + 1) % 4, CS)
                    if f % 2 == 1:
                        rel_piece(h)
                    it = tF()
                    s.op("act", lambda e, bG=bG, it=it: e.activation(out=tmpF.ap[:, it, 0:nq], in_=psum[bG][:, 0:nq], func=AF.Silu),
                         [PR(bG)], [tmpF.res(it)])
                    s.op("dve", lambda e, it=it, bU=bU, f=f: e.tensor_tensor(
                        out=hff.ap[:, f, CS], in0=tmpF.ap[:, it, 0:nq], in1=psum[bU][:, 0:nq], op=ALU.mult),
                        [tmpF.res(it), PR(bU)], [hff.res(f)])

                def down_chunk(c):
                    wap, wres, h = get_piece("d", l, c)
                    b = bank()
                    for f in range(NF):
                        mm(psum[b][:, 0:nq], wap[:, f * 128:(f + 1) * 128], hff.ap[:, f, CS], f == 0, f == NF - 1,
                           [wres, hff.res(f)], [PR(b)], f == NF - 1)
                    rel_piece(h)
                    s.op("dve", lambda e: e.tensor_tensor(
                        out=xT.ap[:, c, cols], in0=psum[b][:, 0:nq], in1=xT.ap[:, c, cols], op=ALU.add),
                        [PR(b), xT.res(c)], [xT.res(c)])
                for c in range(8):
                    pending.append(lambda c=c: down_chunk(c))
            if nxt is not None:
                nl, nt = nxt
                nkb0, _ = tile_cols(nl, nt)
                NKS = slice(nkb0 * 128, 512)
                rms_a(nl, nt, NKS)
                filler(2)
                rms_b(nl, nt, "g1", NKS, nt == 0)
                filler(1)
            else:
                flush_pending()

        def main_body():
            for p in range(P):
                for c in range(8):
                    s.dma(lambda e, p=p, c=c: e.dma_start(out=xT.ap[:, c, :], in_=xT_d[p, :, c, :]), [], [xT.res(c)], "xld%d" % c)
                s.dma(lambda e, p=p: e.dma_start(out=kbias.ap, in_=kbias_d[p, :, :]), [], [kbias.res()], "kbld")
                s.dma(lambda e, p=p: e.dma_start(out=valid.ap, in_=valid_d[p, :, :]), [], [valid.res()], "vdld")
                order = [(l, t) for l in range(L) for t in range(NT)]
                kb0, _ = tile_cols(0, 0)
                rms_a(0, 0, slice(kb0 * 128, 512))
                rms_b(0, 0, "g1", slice(kb0 * 128, 512), True)
                for k, (l, t) in enumerate(order):
                    cur_in["j"] = None
                    if p == 0 and t == 1 and l + 1 < L:
                        emit_casts(l + 1)
                    mixer_tile(p, l, t)
                    if l == 0:
                        rope_gen(p, t + 1)
                    elif l == 1 and t == 0:
                        rope_gen(p + 1, 0)
                    ffn_tile(p, l, t, order[k + 1] if k + 1 < len(order) else None)
                flush_pending()
                for c in range(8):
                    s.dma(lambda e, p=p, c=c: e.dma_start(out=yT_d[p, :, c, :], in_=xT.ap[:, c, HALO:NTOK]), [xT.res(c)],
                          [("yT", p, c)], "yst%d" % c)

        s.dry = True
        main_body()
        s.dry = False
        rope_done.clear()
        st.update(bank=0, tf=0, tb=0)
        cur_in["j"] = None
        prologue()
        main_body()
        assert pst["next"] == len(pst["seq"])
        s.final_wait(["yst%d" % c for c in range(8)])
        build_program.stats = dict(cnt={k: v for k, v in s.cnt.items() if k in ("pe", "act", "dve", "pool")}, nwait=s.nwait,
                                   sbuf=total, npieces=len(pst["seq"]))
    return nc


def host_weights(L, w_in, w_conv_out, w_out, w_gate_up, w_down):
    ch = _win_chunk_cols()
    w_in_img = np.zeros((L, 9, 128, 8, 4, 128), np.float32)
    for j, cols in enumerate(ch):
        if cols is None:
            continue
        blk = w_in[:L][:, :, cols].reshape(L, 8, 128, 128)
        w_in_img[:, j // 4, :, :, j % 4, :] = blk.transpose(0, 2, 1, 3)
    w_in_img = w_in_img.reshape(L, 9, 128, 4096)
    w_co_img = np.ascontiguousarray(w_conv_out[:L].reshape(L, 4, 128, 1024).transpose(0, 2, 1, 3)).reshape(L, 1, 128, 4096)
    w_o_img = np.ascontiguousarray(w_out[:L].reshape(L, 8, 128, 2, 512).transpose(0, 3, 2, 1, 4)).reshape(L, 2, 128, 4096)
    gu = np.zeros((L, 11, 128, 8, 4, 128), np.float32)
    for f in range(NF):
        for which in range(2):
            j = 2 * f + which
            cols = np.arange(which * DFF + f * 128, which * DFF + (f + 1) * 128)
            blk = w_gate_up[:L][:, :, cols].reshape(L, 8, 128, 128)
            gu[:, j // 4, :, :, j % 4, :] = blk.transpose(0, 2, 1, 3)
    w_gu_img = gu.reshape(L, 11, 128, 4096)
    w_d_img = np.ascontiguousarray(w_down[:L].reshape(L, NF, 128, 8, 128).transpose(0, 3, 2, 1, 4)).reshape(L, 8, 128, NF * 128)
    return dict(w_in_img=w_in_img, w_co_img=w_co_img, w_o_img=w_o_img, w_gu_img=w_gu_img, w_d_img=w_d_img)


def host_params(L, norm_mix, q_norm, k_norm, sinks, conv_w, conv_b, conv_ln_g, conv_ln_b, norm_ffn):
    PO, NPAR = par_off(L)
    prm = np.zeros((128, NPAR), np.float32)
    pidx = np.arange(128)
    d = pidx % 64
    partner = np.where(d < 8, d + 8, np.where(d < 16, d - 8, d))
    for l in range(L):
        prm[:, PO["g1"] + l * 8: PO["g1"] + (l + 1) * 8] = norm_mix[l].reshape(8, 128).T
        prm[:, PO["g2"] + l * 8: PO["g2"] + (l + 1) * 8] = norm_ffn[l].reshape(8, 128).T
        prm[:, PO["gq"] + l] = q_norm[l][d]
        prm[:, PO["gqp"] + l] = q_norm[l][partner]
        prm[:, PO["gk"] + l] = k_norm[l][d]
        prm[:, PO["gkp"] + l] = k_norm[l][partner]
        for kh in range(2):
            for ci in range(4):
                prm[:, PO["sink"] + l * 8 + kh * 4 + ci] = sinks[l][kh * 8 + 2 * ci + (pidx // 64)]
        for cc in range(4):
            prm[:, PO["cw"] + (l * 4 + cc) * 31: PO["cw"] + (l * 4 + cc + 1) * 31] = conv_w[l][:, cc * 128:(cc + 1) * 128].T
            prm[:, PO["cb"] + l * 4 + cc] = conv_b[l][cc * 128:(cc + 1) * 128]
            prm[:, PO["lg"] + l * 4 + cc] = conv_ln_g[l][cc * 128:(cc + 1) * 128]
            prm[:, PO["lb"] + l * 4 + cc] = conv_ln_b[l][cc * 128:(cc + 1) * 128]
    fi = (d % 8).astype(np.float64)
    invf = (ROPE_THETA ** (-(2.0 * fi) / 16.0)) / (2.0 * np.pi)
    rot = (d < 16)
    prm[:, PO["invf"]] = np.where(rot, invf, 0.0).astype(np.float32)
    prm[:, PO["rm"]] = rot.astype(np.float32)
    prm[:, PO["onem"]] = 1.0 - rot.astype(np.float32)
    prm[:, PO["mhalf"]] = -0.5
    prm[:, PO["eps"]] = EPS
    prm[:, PO["eps64"]] = HD * EPS
    cm = np.zeros((128, 4, 128), np.float32)
    cm[:, 0, :] = 1.0
    cm[:, 1, :] = (pidx[:, None] // 64 == pidx[None, :] // 64).astype(np.float32)
    for base in (0, 64):
        for dd in range(8):
            cm[base + dd + 8, 2, base + dd] = -1.0
            cm[base + dd, 2, base + dd + 8] = 1.0
    cm[:, 3, :] = np.eye(128, dtype=np.float32)
    return prm, cm.reshape(128, 512)


def make_core_inputs(x_b, half, P, NT, seq_start=None):
    NTOK = NT * TS
    NB = NT * 4
    R = NTOK - HALO
    xT = np.zeros((P, 128, 8, NTOK), np.float32)
    pos = np.zeros((P, 128, NTOK), np.float32)
    kb = np.zeros((P, 128, NB + 1), np.float32)
    vd = np.ones((P, 128, 512), np.float32)
    for p in range(P):
        start = half * P * R + p * R
        lo = start - HALO
        tok = np.arange(lo, start + R)
        pos[p] = tok.astype(np.float32)[None, :]
        ok = tok >= 0
        seg = np.zeros((NTOK, D), np.float32)
        seg[ok] = x_b[tok[ok]]
        xT[p] = seg.T.reshape(8, 128, NTOK).transpose(1, 0, 2)
        kb[p, :, 0] = NEG
        for b in range(NB):
            if lo + b * 128 < 0:
                kb[p, :, b + 1] = NEG
        if lo < 0:
            vd[p] = 0.0
    return dict(xT=xT, pos=pos, kbias=kb, valid=vd)


_CACHE = {}


def run(x, weights, P, NT, L, core_ids, trace=False):
    B, T, _ = x.shape
    R = (NT * TS - HALO)
    key = (P, NT, L)
    if key not in _CACHE:
        _CACHE[key] = build_program(P, NT, L)
    nc = _CACHE[key]
    wi = host_weights(L, weights["w_in"], weights["w_conv_out"], weights["w_out"], weights["w_gate_up"], weights["w_down"])
    prm, cm = host_params(L, weights["norm_mix"], weights["q_norm"], weights["k_norm"], weights["sinks"], weights["conv_w"],
                          weights["conv_b"], weights["conv_ln_g"], weights["conv_ln_b"], weights["norm_ffn"])
    in_maps = []
    nhalf = T // (P * R)
    for cid in core_ids:
        b, half = cid // nhalf, cid % nhalf
        m = make_core_inputs(x[b], half, P, NT)
        m.update(wi)
        m["params"] = prm
        m["cmat"] = cm
        in_maps.append(m)
    res = run_bass_kernel_spmd(nc, in_maps, core_ids=list(range(len(core_ids))))
    out = np.zeros((B, T, D), np.float32)
    for k, cid in enumerate(core_ids):
        b, half = cid // nhalf, cid % nhalf
        yT = res.results[k]["yT"]
        for p in range(P):
            start = half * P * R + p * R
            out[b, start:start + R] = yT[p].transpose(1, 0, 2).reshape(D, R).T
    return out


def kernel(x, norm_mix, w_in, q_norm, k_norm, sinks, conv_w, conv_b, conv_ln_g, conv_ln_b, w_conv_out, w_out,
           norm_ffn, w_gate_up, w_down):
    f = lambda a: np.ascontiguousarray(np.asarray(a, dtype=np.float32))
    weights = dict(norm_mix=f(norm_mix), w_in=f(w_in), q_norm=f(q_norm), k_norm=f(k_norm), sinks=f(sinks), conv_w=f(conv_w),
                   conv_b=f(conv_b), conv_ln_g=f(conv_ln_g), conv_ln_b=f(conv_ln_b), w_conv_out=f(w_conv_out), w_out=f(w_out),
                   norm_ffn=f(norm_ffn), w_gate_up=f(w_gate_up), w_down=f(w_down))
    return run(f(x), weights, P=2, NT=5, L=4, core_ids=list(range(8)))
```

```python
import numpy as np
from contextlib import ExitStack
import concourse.bass as bass
import concourse.mybir as mybir
from concourse.bass_utils import run_bass_kernel_spmd

F32 = mybir.dt.float32
BF16 = mybir.dt.bfloat16
I32 = mybir.dt.int32
AF = mybir.ActivationFunctionType
ALU = mybir.AluOpType

D = 1024
KC = 8
TS = 512
HALO = 512
NH, NKV, HD = 16, 2, 64
CONV_CH, CONV_W = 512, 31
DFF = 2816
NF = DFF // 128
EPS = 1e-6
ROPE_THETA = 500000.0
NSLOT = 4
NTF, NTB = 6, 4
NEG = -30000.0

Q0, K0, V0, U0, UG0, GA0, GB0 = 0, 1024, 1152, 1280, 1792, 2304, 3328


def _win_chunk_cols():
    ch = []
    ch.append(np.concatenate([np.arange(K0, K0 + 64), np.arange(K0, K0 + 64)]))
    ch.append(np.concatenate([np.arange(K0 + 64, K0 + 128), np.arange(K0 + 64, K0 + 128)]))
    ch.append(np.arange(V0, V0 + 128))
    for c in range(4):
        ch.append(np.arange(U0 + c * 128, U0 + (c + 1) * 128))
        ch.append(np.arange(UG0 + c * 128, UG0 + (c + 1) * 128))
    ch.append(None)
    for c in range(8):
        ch.append(np.arange(Q0 + c * 128, Q0 + (c + 1) * 128))
    for c in range(8):
        ch.append(np.arange(GA0 + c * 128, GA0 + (c + 1) * 128))
    for c in range(8):
        ch.append(np.arange(GB0 + c * 128, GB0 + (c + 1) * 128))
    assert len(ch) == 36
    return ch


CH_KD, CH_V, CH_U, CH_PADU, CH_Q, CH_GA, CH_GB = 0, 2, 3, 11, 12, 20, 28


def par_off(L):
    o = {}
    n = 0
    for name, sz in (("g1", L * 8), ("g2", L * 8), ("gq", L), ("gqp", L), ("gk", L), ("gkp", L),
                     ("sink", L * 8), ("cw", L * 4 * 31), ("cb", L * 4), ("lg", L * 4), ("lb", L * 4),
                     ("invf", 1), ("rm", 1), ("onem", 1), ("mhalf", 1), ("eps", 1), ("eps64", 1)):
        o[name] = n
        n += sz
    return o, n


class Sched:
    def __init__(self, nc, es):
        self.nc = nc
        self.es = es
        self.E = {"pe": nc.tensor, "act": nc.scalar, "dve": nc.vector, "pool": nc.gpsimd, "sp": nc.sync}
        self.sem = {}
        self.cnt = {}
        for e in ("pe", "act", "dve", "pool"):
            self.sem[e] = es.enter_context(nc.semaphore("s_" + e))
            self.cnt[e] = 0
        self.seen = {e: {} for e in self.E}
        self.W = {}
        self.R = {}
        self.nwait = 0
        self.dry = False

    def dma_sem(self, name):
        if name not in self.sem:
            self.sem[name] = self.es.enter_context(self.nc.semaphore("d_" + name))
            self.cnt[name] = 0
        return name

    @staticmethod
    def _cells(r):
        if isinstance(r, tuple) and r and r[0] == "sb":
            return [("c", i) for i in range(r[1] // 512, (r[2] + 511) // 512)]
        return [r]

    def _emit(self, eng, fn, reads, writes, evkey, inc, signal=True):
        if self.dry:
            return None
        raw = {}
        oth = {}
        rc = []
        wc = []
        for r in reads:
            for c in self._cells(r):
                rc.append(c)
                for k, v in self.W.get(c, {}).items():
                    if raw.get(k, 0) < v:
                        raw[k] = v
        for w in writes:
            for c in self._cells(w):
                wc.append(c)
                for dct in (self.W, self.R):
                    for k, v in dct.get(c, {}).items():
                        if oth.get(k, 0) < v:
                            oth[k] = v
        need = dict(raw)
        for k, v in oth.items():
            if k == eng and eng == "pe":
                continue
            if need.get(k, 0) < v:
                need[k] = v
        if eng == "pe":
            need.pop("pe", None)
        seen = self.seen[eng]
        for k, v in need.items():
            if seen.get(k, 0) >= v:
                continue
            self.E[eng].wait_ge(self.sem[k], v)
            seen[k] = v
            self.nwait += 1
        inst = fn(self.E[eng])
        ev = self.cnt[evkey] + inc
        if signal:
            inst.then_inc(self.sem[evkey], inc)
            self.cnt[evkey] = ev
        for c in rc:
            d = self.R.setdefault(c, {})
            if d.get(evkey, 0) < ev:
                d[evkey] = ev
        for c in wc:
            d = self.W.setdefault(c, {})
            if d.get(evkey, 0) < ev:
                d[evkey] = ev
        return inst

    def op(self, eng, fn, reads=(), writes=(), signal=True):
        return self._emit(eng, fn, reads, writes, eng, 1, signal)

    def dma(self, fn, reads, writes, sem, eng="sp"):
        if self.dry:
            return None
        self.dma_sem(sem)
        return self._emit(eng, fn, reads, writes, sem, 16, True)

    def final_wait(self, sems):
        for k in sems:
            if k in self.cnt and self.cnt[k] > 0:
                self.E["sp"].wait_ge(self.sem[k], self.cnt[k])


class Buf:
    def __init__(self, arena, off, dtype, shape):
        esz = 2 if dtype == BF16 else 4
        n = int(np.prod(shape))
        self.off = off
        self.nbytes = n * esz
        v = arena[:, off // 2:(off + n * esz) // 2]
        if dtype != BF16:
            v = v.bitcast(dtype)
        if len(shape) == 2:
            v = v.rearrange("p (a b) -> p a b", b=shape[1])
        elif len(shape) == 3:
            v = v.rearrange("p (a b c) -> p a b c", b=shape[1], c=shape[2])
        self.ap = v
        self.cb = self.nbytes // shape[0] if len(shape) > 1 else self.nbytes

    def res(self, i=None, n=1):
        if i is None:
            return ("sb", self.off, self.off + self.nbytes)
        return ("sb", self.off + i * self.cb, self.off + (i + n) * self.cb)


def build_program(P, NT, L):
    NTOK = NT * TS
    NB = NT * 4
    NOUT = NTOK - HALO
    PO, NPAR = par_off(L)
    nc = bass.Bass("TRN2", target_bir_lowering=False)

    def din(name, shape, dt=F32):
        return nc.dram_tensor(name, shape, dt, kind="ExternalInput").ap()

    xT_d = din("xT", [P, 128, 8, NTOK])
    wimg = {
        "in": din("w_in_img", [L, 9, 128, 4096]),
        "co": din("w_co_img", [L, 1, 128, 4096]),
        "o": din("w_o_img", [L, 2, 128, 4096]),
        "gu": din("w_gu_img", [L, 11, 128, 4096]),
        "d": din("w_d_img", [L, 8, 128, 2816]),
    }
    par_d = din("params", [128, NPAR])
    cmat_d = din("cmat", [128, 4 * 128])
    pos_d = din("pos", [P, 128, NTOK])
    kbias_d = din("kbias", [P, 128, NB + 1])
    valid_d = din("valid", [P, 128, 512])
    yT_d = nc.dram_tensor("yT", [P, 128, 8, NOUT], F32, kind="ExternalOutput").ap()
    wscr = {k: nc.dram_tensor("scr_" + k, list(v.shape), BF16, kind="Internal").ap() for k, v in wimg.items()}
    wscr["cd"] = nc.dram_tensor("scr_cd", [L, 4, 128, 31 * 128], BF16, kind="Internal").ap()
    rope_scr = nc.dram_tensor("rope_scr", [P, NT, 128, 1024], F32, kind="Internal").ap()
    NPIECE = {"in": 9, "co": 1, "o": 2, "gu": 11, "d": 8, "cd": 4}
    PCOLS = {"in": 4096, "co": 4096, "o": 4096, "gu": 4096, "d": 2816, "cd": 31 * 128}

    with ExitStack() as es:
        sizes = [
            ("xT", 8 * NTOK * 4), ("wring", NSLOT * 8192),
            ("hT", 8192), ("A8", 8192), ("F16", 16384), ("yF", 8192), ("GA8", 8192), ("aT", 4 * 542 * 2 + 32), ("zb", 4096),
            ("kT", 2 * 640 * 2), ("vb", 5 * 128 * 2), ("Eb", 8192), ("tmpF", NTF * 2048), ("tmpB", NTB * 1024),
            ("rope", 4096), ("par", ((NPAR * 4 + 63) // 64) * 64), ("cmb", 1024), ("sinke", max(64, L * 8 * 4)),
            ("kbias", ((NB + 1) * 4 + 63) // 64 * 64), ("valid", 2048),
        ]
        offs = {}
        o = 0
        for n_, s_ in sizes:
            offs[n_] = o
            o += (s_ + 511) // 512 * 512
        total = o
        arena_t = es.enter_context(nc.sbuf_tensor("arena", [128, total // 2], BF16))
        arena = arena_t[:, :]
        xT = Buf(arena, offs["xT"], F32, [8, NTOK])
        wring = Buf(arena, offs["wring"], BF16, [NSLOT, 4096])
        hT = Buf(arena, offs["hT"], BF16, [8, 512])
        A8 = Buf(arena, offs["A8"], BF16, [8, 512])
        ma = Buf(arena, offs["F16"], F32, [8, 512])
        yF = Buf(arena, offs["yF"], F32, [4, 512])
        hff = Buf(arena, offs["F16"], BF16, [24, 512])
        assert offs["yF"] == offs["F16"] + 16384
        GA8 = Buf(arena, offs["GA8"], BF16, [8, 512])
        aT = Buf(arena, offs["aT"], BF16, [4, 542])
        zb = Buf(arena, offs["zb"], BF16, [4, 512])
        kT = Buf(arena, offs["kT"], BF16, [2, 640])
        vb = Buf(arena, offs["vb"], BF16, [5, 128])
        Eb = Buf(arena, offs["Eb"], BF16, [4, 1024])
        tmpF = Buf(arena, offs["tmpF"], F32, [NTF, 512])
        tmpFi = Buf(arena, offs["tmpF"], I32, [NTF, 512])
        tmpB = Buf(arena, offs["tmpB"], BF16, [NTB, 512])
        rope = Buf(arena, offs["rope"], F32, [2, 512])
        par = Buf(arena, offs["par"], F32, [NPAR])
        cmb = Buf(arena, offs["cmb"], BF16, [4, 128])
        sinke = Buf(arena, offs["sinke"], F32, [L * 8])
        kbias = Buf(arena, offs["kbias"], F32, [NB + 1])
        valid = Buf(arena, offs["valid"], F32, [512])
        stgB = Buf(arena, offs["hT"], BF16, [2, 4096])

        psum = [es.enter_context(nc.psum_tensor("ps%d" % i, [128, 512], F32)) for i in range(8)]
        s = Sched(nc, es)
        st = {"bank": 0, "tf": 0, "tb": 0}

        reserved = set()

        def bank():
            b = st["bank"]
            while b in reserved:
                b = (b + 1) % 8
            st["bank"] = (b + 1) % 8
            return b

        def PR(b):
            return ("ps", b)

        def tF():
            i = st["tf"]
            st["tf"] = (i + 1) % NTF
            return i

        def tB():
            i = st["tb"]
            st["tb"] = (i + 1) % NTB
            return i

        def pcol(name, idx=0):
            c = PO[name] + idx
            return par.ap[:, c:c + 1]

        ones_b = cmb.ap[:, 0, :]
        bd_b = cmb.ap[:, 1, :]
        pm_b = cmb.ap[:, 2, :]
        id_b = cmb.ap[:, 3, :]
        CMR = cmb.res()
        PARR = par.res()

        pst = {"seq": [], "next": 0, "issued": 0, "rel": set()}

        def get_piece(kind, l, i):
            key = (kind, l, i)
            if s.dry:
                pst["seq"].append(key)
                return wring.ap[:, 0, :], wring.res(0), len(pst["seq"]) - 1
            seq = pst["seq"]
            j = pst["next"]
            assert seq[j] == key, (seq[j], key)
            pst["next"] = j + 1
            while pst["issued"] < len(seq) and pst["issued"] <= j + NSLOT - 1:
                q = pst["issued"]
                if q - NSLOT >= 0 and (q - NSLOT) not in pst["rel"]:
                    break
                kq, lq, iq = seq[q]
                sl = q % NSLOT
                nco = PCOLS[kq]
                s.dma(lambda e, sl=sl, nco=nco, kq=kq, lq=lq, iq=iq: e.dma_start(out=wring.ap[:, sl, 0:nco],
                                                                                   in_=wscr[kq][lq, iq, :, :]),
                      [("scr", kq, lq, iq)], [wring.res(sl)], "wslot%d" % sl)
                pst["issued"] = q + 1
            assert pst["issued"] > j, ("ring overflow", key)
            sl = j % NSLOT
            return wring.ap[:, sl, :], wring.res(sl), j

        def rel_piece(j):
            if not s.dry:
                pst["rel"].add(j)

        def mm(out, lhsT, rhs, start, stop, reads, writes, signal, **kw):
            s.op("pe", lambda e: e.matmul(out, lhsT, rhs, start=start, stop=stop, **kw), reads, writes, signal=signal)

        def emit_casts(l):
            for kind in ("in", "co", "o", "gu", "d"):
                npc = NPIECE[kind]
                sem = "cv_%s_%d" % (kind, l)
                src = wimg[kind][l].rearrange("a p c -> (a p) c")
                dst = wscr[kind][l].rearrange("a p c -> (a p) c")
                s.dma(lambda e, src=src, dst=dst: e.dma_start(out=dst, in_=src, max_dma_last_dim=4096), [],
                      [("scr", kind, l, i) for i in range(npc)], sem, eng="pool")

        def prologue():
            s.dma(lambda e: e.dma_start(out=par.ap, in_=par_d[:, :]), [], [PARR], "par")
            i0 = tF()
            tF()
            cm_stage = tmpF.ap[:, i0:i0 + 2, :].rearrange("p a b -> p (a b)")[:, 0:512]
            s.dma(lambda e: e.dma_start(out=cm_stage, in_=cmat_d[:, :]), [], [tmpF.res(i0, 2)], "cm")
            s.op("dve", lambda e: e.tensor_copy(out=cmb.ap.rearrange("p a b -> p (a b)"), in_=cm_stage),
                 [tmpF.res(i0, 2)], [CMR])
            s.op("act", lambda e: e.activation(out=sinke.ap, in_=par.ap[:, PO["sink"]:PO["sink"] + L * 8], func=AF.Exp),
                 [PARR], [sinke.res()])
            emit_casts(0)
            for l in range(L):
                for cc in range(4):
                    sl = (l * 4 + cc) % 2
                    wo = PO["cw"] + (l * 4 + cc) * 31
                    dst3 = stgB.ap[:, sl, 0:31 * 128].rearrange("p (a b) -> p a b", b=128)
                    s.op("dve", lambda e, dst3=dst3, wo=wo: e.tensor_tensor(
                        out=dst3, in0=id_b.unsqueeze(1).broadcast_to([128, 31, 128]),
                        in1=par.ap[:, wo:wo + 31].unsqueeze(2).broadcast_to([128, 31, 128]), op=ALU.mult),
                        [CMR, PARR], [stgB.res(sl)])
                    s.dma(lambda e, sl=sl, l=l, cc=cc: e.dma_start(out=wscr["cd"][l, cc, :, :], in_=stgB.ap[:, sl, 0:31 * 128]),
                          [stgB.res(sl)], [("scr", "cd", l, cc)], "cdst%d" % sl)
            TWO_PI = 6.283185
            for p in range(P):
                for t in range(NT):
                    iu, ir, im = tF(), tF(), tF()
                    U = tmpF.ap[:, iu, :]
                    Rr = tmpF.ap[:, ir, :]
                    Mm = tmpF.ap[:, im, :]
                    Ii = tmpFi.ap[:, im, :]
                    rt = rope.ap
                    s.dma(lambda e, U=U, p=p, t=t: e.dma_start(out=U, in_=pos_d[p, :, t * TS:(t + 1) * TS]), [],
                          [tmpF.res(iu)], "pos%d" % iu)
                    s.op("dve", lambda e, U=U: e.tensor_scalar(out=U, in0=U, scalar1=pcol("invf"), scalar2=None,
                                                                 op0=ALU.mult), [tmpF.res(iu), PARR], [tmpF.res(iu)])
                    for which in (0, 1):
                        sh = 0.25 if which == 0 else 0.0
                        s.op("dve", lambda e, U=U, Rr=Rr, sh=sh: e.tensor_scalar(out=Rr, in0=U, scalar1=sh, scalar2=None,
                                                                                  op0=ALU.add), [tmpF.res(iu)], [tmpF.res(ir)])
                        s.op("dve", lambda e, Ii=Ii, Rr=Rr: e.tensor_copy(out=Ii, in_=Rr), [tmpF.res(ir)], [tmpF.res(im)])
                        s.op("dve", lambda e, Ii=Ii, Mm=Mm: e.tensor_copy(out=Mm, in_=Ii), [tmpF.res(im)], [tmpF.res(im)])
                        s.op("dve", lambda e, Rr=Rr, Mm=Mm: e.tensor_tensor(out=Rr, in0=Rr, in1=Mm, op=ALU.subtract),
                             [tmpF.res(ir), tmpF.res(im)], [tmpF.res(ir)])
                        s.op("dve", lambda e, Rr=Rr, Mm=Mm: e.tensor_scalar(out=Mm, in0=Rr, scalar1=0.5, scalar2=None,
                                                                             op0=ALU.is_gt), [tmpF.res(ir)], [tmpF.res(im)])
                        s.op("dve", lambda e, Rr=Rr, Mm=Mm: e.tensor_tensor(out=Rr, in0=Rr, in1=Mm, op=ALU.subtract),
                             [tmpF.res(ir), tmpF.res(im)], [tmpF.res(ir)])
                        s.op("dve", lambda e, Rr=Rr, Mm=Mm: e.tensor_scalar(out=Mm, in0=Rr, scalar1=-0.5, scalar2=None,
                                                                             op0=ALU.is_lt), [tmpF.res(ir)], [tmpF.res(im)])
                        s.op("dve", lambda e, Rr=Rr, Mm=Mm: e.tensor_tensor(out=Rr, in0=Rr, in1=Mm, op=ALU.add),
                             [tmpF.res(ir), tmpF.res(im)], [tmpF.res(ir)])
                        s.op("act", lambda e, Rr=Rr: e.activation(out=Rr, in_=Rr, func=AF.Sin, scale=TWO_PI),
                             [tmpF.res(ir)], [tmpF.res(ir)])
                        if which == 0:
                            s.op("dve", lambda e, Rr=Rr: e.tensor_scalar(out=rt[:, 0, :], in0=Rr, scalar1=pcol("rm"),
                                                                          scalar2=pcol("onem"), op0=ALU.mult, op1=ALU.add),
                                 [tmpF.res(ir), PARR], [rope.res(0)])
                        else:
                            s.op("dve", lambda e, Rr=Rr: e.tensor_scalar(out=rt[:, 1, :], in0=Rr, scalar1=pcol("rm"),
                                                                          scalar2=None, op0=ALU.mult),
                                 [tmpF.res(ir), PARR], [rope.res(1)])
                    s.dma(lambda e, p=p, t=t: e.dma_start(out=rope_scr[p, t, :, :], in_=rope.ap.rearrange("p a b -> p (a b)")),
                          [rope.res()], [("ropescr", p, t)], "ropest")

        def tile_cols(l, t):
            if t == 0:
                kb0 = min(l, 4)
                qb0 = min(l + 1, 4)
            else:
                kb0 = qb0 = 0
            return kb0, qb0

        def pow_rs(ir, n, expo=-0.5):
            rs = tmpF.ap[:, ir, 0:n]
            s.op("dve", lambda e: e.reciprocal(out=rs, in_=rs), [tmpF.res(ir)], [tmpF.res(ir)])

        def rms_a(l, t, cs):
            cols = slice(t * TS + cs.start, (t + 1) * TS)
            for c in range(8):
                s.op("act", lambda e, c=c: e.activation(out=A8.ap[:, c, cs], in_=xT.ap[:, c, cols], func=AF.Square),
                     [xT.res(c)], [A8.res(c)])

        def rms_b(l, t, gname, cs, use_valid):
            cols = slice(t * TS + cs.start, (t + 1) * TS)
            n = 512 - cs.start
            b = bank()
            for c in range(8):
                mm(psum[b][:, 0:n], ones_b, A8.ap[:, c, cs], c == 0, c == 7, [CMR, A8.res(c)], [PR(b)], c == 7)
            ir = tF()
            rs = tmpF.ap[:, ir, 0:n]
            s.op("act", lambda e: e.activation(out=rs, in_=psum[b][:, 0:n], func=AF.Sqrt, scale=1.0 / D, bias=pcol("eps")),
                 [PR(b), PARR], [tmpF.res(ir)])
            pow_rs(ir, n)
            if use_valid:
                s.op("dve", lambda e: e.tensor_tensor(out=rs, in0=rs, in1=valid.ap[:, cs], op=ALU.mult),
                     [tmpF.res(ir), valid.res()], [tmpF.res(ir)])
            for c in range(8):
                s.op("dve", lambda e, c=c: e.scalar_tensor_tensor(out=hT.ap[:, c, cs], in0=xT.ap[:, c, cols],
                                                                  scalar=pcol(gname, l * 8 + c), in1=rs,
                                                                  op0=ALU.mult, op1=ALU.mult),
                     [xT.res(c), PARR, tmpF.res(ir)], [hT.res(c)])

        def proj_chunk(wap, wres, pos, cs, src=None, kcn=8, wstride=512):
            src = hT if src is None else src
            n = 512 - cs.start
            b = bank()
            for kc in range(kcn):
                lhsT = wap[:, kc * wstride + pos * 128: kc * wstride + (pos + 1) * 128]
                mm(psum[b][:, 0:n], lhsT, src.ap[:, kc, cs], kc == 0, kc == kcn - 1, [wres, src.res(kc)], [PR(b)],
                   kc == kcn - 1)
            return b

        def qk_P(bA, cs, slot):
            n = 512 - cs.start
            A = psum[bA][:, 0:n]
            if slot < 2:
                sq_ap, sq_res = tmpB.ap[:, slot * 2, 0:n], tmpB.res(slot * 2)
                raw_ap, raw_res = tmpB.ap[:, slot * 2 + 1, 0:n], tmpB.res(slot * 2 + 1)
            else:
                sq_ap, sq_res = zb.ap[:, 0, 0:n], zb.res(0)
                raw_ap, raw_res = zb.ap[:, 1, 0:n], zb.res(1)
            s.op("act", lambda e: e.activation(out=sq_ap, in_=A, func=AF.Square), [PR(bA)], [sq_res])
            s.op("act", lambda e: e.activation(out=raw_ap, in_=A, func=AF.Copy), [PR(bA)], [raw_res])
            reserved.add(bA)
            return (bA, sq_ap, sq_res, raw_ap, raw_res)

        def qk_H(stt, gname, gpname, l, cs, out_ap, out_res):
            bA, sq_ap, sq_res, raw_ap, raw_res = stt
            n = 512 - cs.start
            A = psum[bA][:, 0:n]
            bB = bank()
            mm(psum[bB][:, 0:n], bd_b, sq_ap, True, True, [CMR, sq_res], [PR(bB)], True)
            bC = bank()
            mm(psum[bC][:, 0:n], pm_b, raw_ap, True, True, [CMR, raw_res], [PR(bC)], True)
            ir = tF()
            rs = tmpF.ap[:, ir, 0:n]
            s.op("act", lambda e: e.activation(out=rs, in_=psum[bB][:, 0:n], func=AF.Sqrt, scale=1.0, bias=pcol("eps64")),
                 [PR(bB), PARR], [tmpF.res(ir)])
            pow_rs(ir, n)
            i1, i2 = tF(), tF()
            t1 = tmpF.ap[:, i1, 0:n]
            t2 = tmpF.ap[:, i2, 0:n]
            s.op("dve", lambda e: e.scalar_tensor_tensor(out=t1, in0=A, scalar=pcol(gname, l), in1=rope.ap[:, 0, cs],
                                                         op0=ALU.mult, op1=ALU.mult),
                 [PR(bA), PARR, rope.res(0)], [tmpF.res(i1)])
            reserved.discard(bA)
            s.op("dve", lambda e: e.scalar_tensor_tensor(out=t2, in0=psum[bC][:, 0:n], scalar=pcol(gpname, l),
                                                         in1=rope.ap[:, 1, cs], op0=ALU.mult, op1=ALU.mult),
                 [PR(bC), PARR, rope.res(1)], [tmpF.res(i2)])
            s.op("pool", lambda e: e.tensor_tensor(out=t1, in0=t1, in1=t2, op=ALU.add),
                 [tmpF.res(i1), tmpF.res(i2)], [tmpF.res(i1)])
            s.op("pool", lambda e: e.tensor_tensor(out=out_ap, in0=t1, in1=rs, op=ALU.mult),
                 [tmpF.res(i1), tmpF.res(ir)], [out_res])

        cur_in = {"j": None}

        def in_chunk(l, j):
            if j % 4 == 0 or cur_in["j"] is None or cur_in["pc"] != j // 4:
                cur_in["wap"], cur_in["wres"], cur_in["h"] = get_piece("in", l, j // 4)
                cur_in["pc"] = j // 4
            cur_in["j"] = j
            return cur_in["wap"], cur_in["wres"], j % 4

        def in_done(j):
            if j % 4 == 3 or j == CH_PADU - 1:
                rel_piece(cur_in["h"])
                cur_in["j"] = None

        def mixer_tile(p, l, t):
            kb0, qb0 = tile_cols(l, t)
            KS = slice(kb0 * 128, 512)
            CS = slice(qb0 * 128, 512)
            nk = 512 - KS.start
            nq = 512 - CS.start
            cols = slice(t * TS + CS.start, (t + 1) * TS)
            s.dma(lambda e: e.dma_start(out=rope.ap.rearrange("p a b -> p (a b)"), in_=rope_scr[p, t, :, :]),
                  [("ropescr", p, t)], [rope.res()], "ropeld")
            if t == 0:
                s.op("pool", lambda e: e.memset(kT.ap[:, :, 0:128], 0.0), [], [kT.res()])
                s.op("pool", lambda e: e.memset(vb.ap[:, 0, :], 0.0), [], [vb.res()])
                s.op("pool", lambda e: e.memset(aT.ap[:, :, 0:30], 0.0), [], [aT.res()])
            else:
                s.op("pool", lambda e: e.tensor_copy(out=kT.ap[:, :, 0:128], in_=kT.ap[:, :, 512:640]), [kT.res()], [kT.res()])
                s.op("pool", lambda e: e.tensor_copy(out=vb.ap[:, 0, :], in_=vb.ap[:, 4, :]), [vb.res()], [vb.res()])
                s.op("pool", lambda e: e.tensor_copy(out=aT.ap[:, :, 0:30], in_=aT.ap[:, :, 512:542]), [aT.res()], [aT.res()])
            stk = []
            for kh in range(2):
                wap, wres, pos = in_chunk(l, CH_KD + kh)
                bA = proj_chunk(wap, wres, pos, KS)
                in_done(CH_KD + kh)
                stk.append(qk_P(bA, KS, kh))
            wap, wres, pos = in_chunk(l, CH_V)
            bV = bank()
            for blk in range(kb0, 4):
                for kc in range(8):
                    mm(psum[bV][:, blk * 128:(blk + 1) * 128], hT.ap[:, kc, blk * 128:(blk + 1) * 128],
                       wap[:, kc * 512 + pos * 128: kc * 512 + (pos + 1) * 128], kc == 0, kc == 7,
                       [wres, hT.res(kc)], [PR(bV)], kc == 7)
            in_done(CH_V)
            s.op("act", lambda e: e.activation(out=vb.ap[:, 1 + kb0:5, :].rearrange("p a b -> p (a b)"), in_=psum[bV][:, KS],
                                               func=AF.Copy), [PR(bV)], [vb.res()])
            for kh in range(2):
                qk_H(stk[kh], "gk", "gkp", l, KS, kT.ap[:, kh, 128 + KS.start:640], kT.res(kh))
            for cc in range(4):
                j = CH_U + 2 * cc
                wap, wres, pos = in_chunk(l, j)
                bU = proj_chunk(wap, wres, pos, KS)
                in_done(j)
                wap, wres, pos = in_chunk(l, j + 1)
                bG = proj_chunk(wap, wres, pos, KS)
                in_done(j + 1)
                if cur_in["j"] is None or cc == 0:
                    filler(1)
                it = tF()
                s.op("act", lambda e, bG=bG, it=it: e.activation(out=tmpF.ap[:, it, 0:nk], in_=psum[bG][:, 0:nk], func=AF.Tanh,
                                                               scale=0.5), [PR(bG)], [tmpF.res(it)])
                s.op("dve", lambda e, it=it, cc=cc, bU=bU: e.scalar_tensor_tensor(
                    out=aT.ap[:, cc, 30 + KS.start:542], in0=tmpF.ap[:, it, 0:nk], scalar=1.0, in1=psum[bU][:, 0:nk],
                    op0=ALU.add, op1=ALU.mult), [tmpF.res(it), PR(bU)], [aT.res(cc)])
            if nq == 0:
                flush_pending()
                return
            conv_st = {}

            def conv_half(cc, half):
                if half == 0:
                    wap, wres, h = get_piece("cd", l, cc)
                    b = bank()
                    reserved.add(b)
                    conv_st[cc] = (wap, wres, h, b)
                wap, wres, h, b = conv_st[cc]
                rng_ = range(0, 16) if half == 0 else range(16, 31)
                for jt in rng_:
                    mm(psum[b][:, 0:nq], wap[:, jt * 128:(jt + 1) * 128], aT.ap[:, cc, jt + CS.start:jt + 512], jt == 0, jt == 30,
                       [wres, aT.res(cc)], [PR(b)], jt == 15 or jt == 30)
                if half == 0:
                    return
                rel_piece(h)
                reserved.discard(b)
                cbo = PO["cb"] + l * 4 + cc
                iy, iq = cc, 4 + cc
                s.op("act", lambda e: e.activation(out=yF.ap[:, cc, CS], in_=psum[b][:, 0:nq], func=AF.Identity,
                                                   scale=0.5, bias=par.ap[:, cbo:cbo + 1]), [PR(b), PARR], [yF.res(cc)])
                s.op("act", lambda e: e.activation(out=zb.ap[:, cc, 0:nq], in_=psum[b][:, 0:nq], func=AF.Identity,
                                                   scale=0.5, bias=par.ap[:, cbo:cbo + 1]), [PR(b), PARR], [zb.res(cc)])
                s.op("act", lambda e: e.activation(out=tmpB.ap[:, cc, 0:nq], in_=psum[b][:, 0:nq], func=AF.Square,
                                                   scale=0.5, bias=par.ap[:, cbo:cbo + 1]), [PR(b), PARR], [tmpB.res(cc)])

            def ln_stats_and_chain():
                bS1 = bank()
                bS2 = bank()
                reserved.update((bS1, bS2))
                for cc in range(4):
                    mm(psum[bS1][:, 0:nq], ones_b, zb.ap[:, cc, 0:nq], cc == 0, cc == 3, [CMR, zb.res(cc)], [PR(bS1)], cc == 3)
                for cc in range(4):
                    mm(psum[bS2][:, 0:nq], ones_b, tmpB.ap[:, cc, 0:nq], cc == 0, cc == 3, [CMR, tmpB.res(cc)], [PR(bS2)], cc == 3)
                imean, ivar = tF(), tF()
                mean = tmpF.ap[:, imean, 0:nq]
                var = tmpF.ap[:, ivar, 0:nq]
                s.op("dve", lambda e: e.tensor_scalar(out=mean, in0=psum[bS1][:, 0:nq], scalar1=1.0 / CONV_CH, scalar2=None, op0=ALU.mult),
                     [PR(bS1)], [tmpF.res(imean)])
                s.op("dve", lambda e: e.tensor_tensor(out=var, in0=mean, in1=mean, op=ALU.mult),
                     [tmpF.res(imean)], [tmpF.res(ivar)])
                s.op("dve", lambda e: e.scalar_tensor_tensor(out=var, in0=psum[bS2][:, 0:nq], scalar=1.0 / CONV_CH, in1=var,
                                                             op0=ALU.mult, op1=ALU.subtract),
                     [PR(bS2), tmpF.res(ivar)], [tmpF.res(ivar)])
                reserved.discard(bS1)
                reserved.discard(bS2)
                s.op("dve", lambda e: e.tensor_scalar(out=var, in0=var, scalar1=0.0, scalar2=EPS, op0=ALU.max, op1=ALU.add),
                     [tmpF.res(ivar)], [tmpF.res(ivar)])
                s.op("act", lambda e: e.activation(out=var, in_=var, func=AF.Sqrt), [tmpF.res(ivar)], [tmpF.res(ivar)])
                pow_rs(ivar, nq)
                for cc in range(4):
                    s.op("dve", lambda e, cc=cc: e.tensor_tensor(out=yF.ap[:, cc, CS], in0=yF.ap[:, cc, CS], in1=mean, op=ALU.subtract),
                         [yF.res(cc), tmpF.res(imean)], [yF.res(cc)])
                    s.op("dve", lambda e, cc=cc: e.tensor_tensor(out=yF.ap[:, cc, CS], in0=yF.ap[:, cc, CS], in1=var, op=ALU.mult),
                         [yF.res(cc), tmpF.res(ivar)], [yF.res(cc)])

            def q_P(c):
                j = CH_Q + c
                wap, wres, pos = in_chunk(l, j)
                bA = proj_chunk(wap, wres, pos, CS)
                in_done(j)
                return qk_P(bA, CS, c % 3)

            stq = [None] * 8
            stq[0] = q_P(0)
            stq[1] = q_P(1)
            for c in range(8):
                if c + 2 < 8:
                    stq[c + 2] = q_P(c + 2)
                if c == 1 or c == 5:
                    filler(2)
                qk_H(stq[c], "gq", "gqp", l, CS, A8.ap[:, c, CS], A8.res(c))
            flush_pending()
            its = [(qb, kh) for qb in range(qb0, 4) for kh in range(2)]

            def att_qk(i):
                qb, kh = its[i]
                qc = slice(qb * 128, (qb + 1) * 128)
                buf = i % 2
                for kbi in range(2):
                    kcs = slice((qb + kbi) * 128, (qb + kbi + 1) * 128)
                    kcol = 4 * t + qb + kbi
                    ei = buf * 2 + kbi
                    for parh in range(2):
                        b = bank()
                        pr = slice(parh * 64, (parh + 1) * 64)
                        for ci in range(4):
                            c = kh * 4 + ci
                            mm(psum[b][:, ci * 128:(ci + 1) * 128], kT.ap[pr, kh, kcs], A8.ap[pr, c, qc], True, True,
                               [kT.res(kh), A8.res(c)], [PR(b)], ci == 3)
                        s.op("act", lambda e, b=b, ei=ei, parh=parh, kcol=kcol: e.activation(
                            out=Eb.ap[:, ei, parh * 512:(parh + 1) * 512], in_=psum[b][:, :], func=AF.Exp, scale=8.0,
                            bias=kbias.ap[:, kcol:kcol + 1]), [PR(b), kbias.res()], [Eb.res(ei)])
                    ev = Eb.ap[:, ei, :].rearrange("p (a b) -> p a b", b=128)
                    if kbi == 1:
                        s.op("pool", lambda e, ev=ev: e.affine_select(out=ev, in_=ev, pattern=[[0, 8], [1, 128]],
                                                                     compare_op=ALU.is_ge, fill=0.0, base=0,
                                                                     channel_multiplier=-1), [Eb.res(ei)], [Eb.res(ei)])
                    else:
                        s.op("pool", lambda e, ev=ev: e.affine_select(out=ev, in_=ev, pattern=[[0, 8], [-1, 128]],
                                                                     compare_op=ALU.is_gt, fill=0.0, base=0,
                                                                     channel_multiplier=1), [Eb.res(ei)], [Eb.res(ei)])

            def att_pv(i):
                qb, kh = its[i]
                qc = slice(qb * 128, (qb + 1) * 128)
                buf = i % 2
                bO = bank()
                bD = bank()
                for (bb, is_den) in ((bO, False), (bD, True)):
                    for parh in range(2):
                        for kbi in range(2):
                            ei = buf * 2 + kbi
                            lhsT = ones_b[:, 0:64] if is_den else vb.ap[:, qb + kbi, kh * 64:(kh + 1) * 64]
                            rd = [CMR] if is_den else [vb.res()]
                            mm(psum[bb][parh * 64:(parh + 1) * 64, :], lhsT, Eb.ap[:, ei, parh * 512:(parh + 1) * 512],
                               kbi == 0, kbi == 1, rd + [Eb.res(ei)], [PR(bb)], (parh == 1 and kbi == 1),
                               tile_position=(0, parh * 64))
                idn = tF()
                dn = tmpF.ap[:, idn, :]
                so = l * 8 + kh * 4
                s.op("dve", lambda e: e.tensor_tensor(
                    out=dn.rearrange("p (a b) -> p a b", b=128), in0=psum[bD][:, :].rearrange("p (a b) -> p a b", b=128),
                    in1=sinke.ap[:, so:so + 4].unsqueeze(2).broadcast_to([128, 4, 128]),
                    op=ALU.add), [PR(bD), sinke.res()], [tmpF.res(idn)])
                pow_rs(idn, 512, -1.0)
                s.op("dve", lambda e: e.tensor_tensor(
                    out=ma.ap[:, kh * 4:(kh + 1) * 4, qc], in0=psum[bO][:, :].rearrange("p (a b) -> p a b", b=128),
                    in1=dn.rearrange("p (a b) -> p a b", b=128), op=ALU.mult),
                    [PR(bO), tmpF.res(idn)], [ma.res(kh * 4, 4)])

            def ga_chunk(c):
                j = CH_GA + c
                wap, wres, pos = in_chunk(l, j)
                b = proj_chunk(wap, wres, pos, CS)
                in_done(j)
                s.op("act", lambda e: e.activation(out=GA8.ap[:, c, CS], in_=psum[b][:, 0:nq], func=AF.Tanh, scale=0.5),
                     [PR(b)], [GA8.res(c)])

            conv_todo = [(cc, h) for cc in range(4) for h in range(2)]
            ga_next = 0
            att_qk(0)
            for i in range(len(its)):
                if i + 1 < len(its):
                    att_qk(i + 1)
                att_pv(i)
                if conv_todo:
                    conv_half(*conv_todo.pop(0))
                if ga_next < 8:
                    ga_chunk(ga_next)
                    ga_next += 1
            while conv_todo:
                conv_half(*conv_todo.pop(0))
            while ga_next < 8:
                ga_chunk(ga_next)
                ga_next += 1
            for c in range(8):
                s.op("dve", lambda e, c=c: e.scalar_tensor_tensor(out=ma.ap[:, c, CS], in0=GA8.ap[:, c, CS], scalar=1.0,
                                                                  in1=ma.ap[:, c, CS], op0=ALU.add, op1=ALU.mult),
                     [ma.res(c), GA8.res(c)], [ma.res(c)])
            ln_stats_and_chain()
            for c in range(8):
                j = CH_GB + c
                wap, wres, pos = in_chunk(l, j)
                bG = proj_chunk(wap, wres, pos, CS)
                in_done(j)
                s.op("act", lambda e, bG=bG, c=c: e.activation(out=GA8.ap[:, c, CS], in_=psum[bG][:, 0:nq], func=AF.Tanh, scale=0.5),
                     [PR(bG)], [GA8.res(c)])
            for cc in range(4):
                lgo = PO["lg"] + l * 4 + cc
                lbo = PO["lb"] + l * 4 + cc
                s.op("act", lambda e, cc=cc, lgo=lgo, lbo=lbo: e.activation(
                    out=zb.ap[:, cc, CS], in_=yF.ap[:, cc, CS], func=AF.Silu, scale=par.ap[:, lgo:lgo + 1],
                    bias=par.ap[:, lbo:lbo + 1]), [yF.res(cc), PARR], [zb.res(cc)])
            cwap, cwres, ch = get_piece("co", l, 0)
            for c in range(8):
                bC = bank()
                for cc in range(4):
                    mm(psum[bC][:, 0:nq], cwap[:, cc * 1024 + c * 128: cc * 1024 + (c + 1) * 128], zb.ap[:, cc, CS], cc == 0, cc == 3,
                       [cwres, zb.res(cc)], [PR(bC)], cc == 3)
                ip = tF()
                s.op("dve", lambda e, ip=ip, bC=bC, c=c: e.scalar_tensor_tensor(
                    out=tmpF.ap[:, ip, 0:nq], in0=GA8.ap[:, c, CS], scalar=1.0, in1=psum[bC][:, 0:nq], op0=ALU.add, op1=ALU.mult),
                    [GA8.res(c), PR(bC)], [tmpF.res(ip)])
                s.op("pool", lambda e, ip=ip, c=c: e.tensor_tensor(
                    out=A8.ap[:, c, CS], in0=tmpF.ap[:, ip, 0:nq], in1=ma.ap[:, c, CS], op=ALU.add),
                    [tmpF.res(ip), ma.res(c)], [A8.res(c)])
            rel_piece(ch)
            for c in range(8):
                if c % 4 == 0:
                    wap, wres, h = get_piece("o", l, c // 4)
                b = proj_chunk(wap, wres, c % 4, CS, src=A8)
                if c % 4 == 3:
                    rel_piece(h)
                s.op("dve", lambda e, b=b, c=c: e.scalar_tensor_tensor(
                    out=xT.ap[:, c, cols], in0=psum[b][:, 0:nq], scalar=0.5, in1=xT.ap[:, c, cols], op0=ALU.mult, op1=ALU.add),
                    [PR(b), xT.res(c)], [xT.res(c)])

        pending = []

        def filler(n=1):
            for _ in range(n):
                if pending:
                    pending.pop(0)()

        def flush_pending():
            while pending:
                pending.pop(0)()

        def ffn_tile(p, l, t, nxt):
            kb0, qb0 = tile_cols(l, t)
            CS = slice(qb0 * 128, 512)
            nq = 512 - CS.start
            cols = slice(t * TS + CS.start, (t + 1) * TS)
            flush_pending()
            if nq > 0:
                rms_a(l, t, CS)
                rms_b(l, t, "g2", CS, False)
                for f in range(NF):
                    if f % 2 == 0:
                        wap, wres, h = get_piece("gu", l, f // 2)
                    bG = proj_chunk(wap, wres, (2 * f) % 4, CS)
                    bU = proj_chunk(wap, wres, (2 * f + 1) % 4, CS)
                    if f % 2 == 1:
                        rel_piece(h)
                    it = tF()
                    s.op("act", lambda e, bG=bG, it=it: e.activation(out=tmpF.ap[:, it, 0:nq], in_=psum[bG][:, 0:nq], func=AF.Silu),
                         [PR(bG)], [tmpF.res(it)])
                    s.op("dve", lambda e, it=it, bU=bU, f=f: e.tensor_tensor(
                        out=hff.ap[:, f, CS], in0=tmpF.ap[:, it, 0:nq], in1=psum[bU][:, 0:nq], op=ALU.mult),
                        [tmpF.res(it), PR(bU)], [hff.res(f)])

                def down_chunk(c):
                    wap, wres, h = get_piece("d", l, c)
                    b = bank()
                    for f in range(NF):
                        mm(psum[b][:, 0:nq], wap[:, f * 128:(f + 1) * 128], hff.ap[:, f, CS], f == 0, f == NF - 1,
                           [wres, hff.res(f)], [PR(b)], f == NF - 1)
                    rel_piece(h)
                    s.op("dve", lambda e: e.tensor_tensor(
                        out=xT.ap[:, c, cols], in0=psum[b][:, 0:nq], in1=xT.ap[:, c, cols], op=ALU.add),
                        [PR(b), xT.res(c)], [xT.res(c)])
                for c in range(8):
                    pending.append(lambda c=c: down_chunk(c))
            if nxt is not None:
                nl, nt = nxt
                nkb0, _ = tile_cols(nl, nt)
                NKS = slice(nkb0 * 128, 512)
                rms_a(nl, nt, NKS)
                filler(2)
                rms_b(nl, nt, "g1", NKS, nt == 0)
                filler(2)
            else:
                flush_pending()

        def main_body():
            for p in range(P):
                for c in range(8):
                    s.dma(lambda e, p=p, c=c: e.dma_start(out=xT.ap[:, c, :], in_=xT_d[p, :, c, :]), [], [xT.res(c)], "xld%d" % c)
                s.dma(lambda e, p=p: e.dma_start(out=kbias.ap, in_=kbias_d[p, :, :]), [], [kbias.res()], "kbld")
                s.dma(lambda e, p=p: e.dma_start(out=valid.ap, in_=valid_d[p, :, :]), [], [valid.res()], "vdld")
                order = [(l, t) for l in range(L) for t in range(NT)]
                kb0, _ = tile_cols(0, 0)
                rms_a(0, 0, slice(kb0 * 128, 512))
                rms_b(0, 0, "g1", slice(kb0 * 128, 512), True)
                for k, (l, t) in enumerate(order):
                    cur_in["j"] = None
                    if p == 0 and t == 1 and l + 1 < L:
                        emit_casts(l + 1)
                    mixer_tile(p, l, t)
                    ffn_tile(p, l, t, order[k + 1] if k + 1 < len(order) else None)
                flush_pending()
                for c in range(8):
                    s.dma(lambda e, p=p, c=c: e.dma_start(out=yT_d[p, :, c, :], in_=xT.ap[:, c, HALO:NTOK]), [xT.res(c)],
                          [("yT", p, c)], "yst%d" % c)

        s.dry = True
        main_body()
        s.dry = False
        st.update(bank=0, tf=0, tb=0)
        cur_in["j"] = None
        prologue()
        main_body()
        assert pst["next"] == len(pst["seq"])
        s.final_wait(["yst%d" % c for c in range(8)])
        build_program.stats = dict(cnt={k: v for k, v in s.cnt.items() if k in ("pe", "act", "dve", "pool")}, nwait=s.nwait,
                                   sbuf=total, npieces=len(pst["seq"]))
    return nc


def host_weights(L, w_in, w_conv_out, w_out, w_gate_up, w_down):
    ch = _win_chunk_cols()
    w_in_img = np.zeros((L, 9, 128, 8, 4, 128), np.float32)
    for j, cols in enumerate(ch):
        if cols is None:
            continue
        blk = w_in[:L][:, :, cols].reshape(L, 8, 128, 128)
        w_in_img[:, j // 4, :, :, j % 4, :] = blk.transpose(0, 2, 1, 3)
    w_in_img = w_in_img.reshape(L, 9, 128, 4096)
    w_co_img = np.ascontiguousarray(w_conv_out[:L].reshape(L, 4, 128, 1024).transpose(0, 2, 1, 3)).reshape(L, 1, 128, 4096)
    w_o_img = np.ascontiguousarray(w_out[:L].reshape(L, 8, 128, 2, 512).transpose(0, 3, 2, 1, 4)).reshape(L, 2, 128, 4096)
    gu = np.zeros((L, 11, 128, 8, 4, 128), np.float32)
    for f in range(NF):
        for which in range(2):
            j = 2 * f + which
            cols = np.arange(which * DFF + f * 128, which * DFF + (f + 1) * 128)
            blk = w_gate_up[:L][:, :, cols].reshape(L, 8, 128, 128)
            gu[:, j // 4, :, :, j % 4, :] = blk.transpose(0, 2, 1, 3)
    w_gu_img = gu.reshape(L, 11, 128, 4096)
    w_d_img = np.ascontiguousarray(w_down[:L].reshape(L, NF, 128, 8, 128).transpose(0, 3, 2, 1, 4)).reshape(L, 8, 128, NF * 128)
    return dict(w_in_img=w_in_img, w_co_img=w_co_img, w_o_img=w_o_img, w_gu_img=w_gu_img, w_d_img=w_d_img)


def host_params(L, norm_mix, q_norm, k_norm, sinks, conv_w, conv_b, conv_ln_g, conv_ln_b, norm_ffn):
    PO, NPAR = par_off(L)
    prm = np.zeros((128, NPAR), np.float32)
    pidx = np.arange(128)
    d = pidx % 64
    partner = np.where(d < 8, d + 8, np.where(d < 16, d - 8, d))
    for l in range(L):
        prm[:, PO["g1"] + l * 8: PO["g1"] + (l + 1) * 8] = norm_mix[l].reshape(8, 128).T
        prm[:, PO["g2"] + l * 8: PO["g2"] + (l + 1) * 8] = norm_ffn[l].reshape(8, 128).T
        prm[:, PO["gq"] + l] = q_norm[l][d]
        prm[:, PO["gqp"] + l] = q_norm[l][partner]
        prm[:, PO["gk"] + l] = k_norm[l][d]
        prm[:, PO["gkp"] + l] = k_norm[l][partner]
        for kh in range(2):
            for ci in range(4):
                prm[:, PO["sink"] + l * 8 + kh * 4 + ci] = sinks[l][kh * 8 + 2 * ci + (pidx // 64)]
        for cc in range(4):
            prm[:, PO["cw"] + (l * 4 + cc) * 31: PO["cw"] + (l * 4 + cc + 1) * 31] = conv_w[l][:, cc * 128:(cc + 1) * 128].T
            prm[:, PO["cb"] + l * 4 + cc] = conv_b[l][cc * 128:(cc + 1) * 128]
            prm[:, PO["lg"] + l * 4 + cc] = conv_ln_g[l][cc * 128:(cc + 1) * 128]
            prm[:, PO["lb"] + l * 4 + cc] = conv_ln_b[l][cc * 128:(cc + 1) * 128]
    fi = (d % 8).astype(np.float64)
    invf = (ROPE_THETA ** (-(2.0 * fi) / 16.0)) / (2.0 * np.pi)
    rot = (d < 16)
    prm[:, PO["invf"]] = np.where(rot, invf, 0.0).astype(np.float32)
    prm[:, PO["rm"]] = rot.astype(np.float32)
    prm[:, PO["onem"]] = 1.0 - rot.astype(np.float32)
    prm[:, PO["mhalf"]] = -0.5
    prm[:, PO["eps"]] = EPS
    prm[:, PO["eps64"]] = HD * EPS
    cm = np.zeros((128, 4, 128), np.float32)
    cm[:, 0, :] = 1.0
    cm[:, 1, :] = (pidx[:, None] // 64 == pidx[None, :] // 64).astype(np.float32)
    for base in (0, 64):
        for dd in range(8):
            cm[base + dd + 8, 2, base + dd] = -1.0
            cm[base + dd, 2, base + dd + 8] = 1.0
    cm[:, 3, :] = np.eye(128, dtype=np.float32)
    return prm, cm.reshape(128, 512)


def make_core_inputs(x_b, half, P, NT, seq_start=None):
    NTOK = NT * TS
    NB = NT * 4
    R = NTOK - HALO
    xT = np.zeros((P, 128, 8, NTOK), np.float32)
    pos = np.zeros((P, 128, NTOK), np.float32)
    kb = np.zeros((P, 128, NB + 1), np.float32)
    vd = np.ones((P, 128, 512), np.float32)
    for p in range(P):
        start = half * P * R + p * R
        lo = start - HALO
        tok = np.arange(lo, start + R)
        pos[p] = tok.astype(np.float32)[None, :]
        ok = tok >= 0
        seg = np.zeros((NTOK, D), np.float32)
        seg[ok] = x_b[tok[ok]]
        xT[p] = seg.T.reshape(8, 128, NTOK).transpose(1, 0, 2)
        kb[p, :, 0] = NEG
        for b in range(NB):
            if lo + b * 128 < 0:
                kb[p, :, b + 1] = NEG
        if lo < 0:
            vd[p] = 0.0
    return dict(xT=xT, pos=pos, kbias=kb, valid=vd)


_CACHE = {}


def run(x, weights, P, NT, L, core_ids, trace=False):
    B, T, _ = x.shape
    R = (NT * TS - HALO)
    key = (P, NT, L)
    if key not in _CACHE:
        _CACHE[key] = build_program(P, NT, L)
    nc = _CACHE[key]
    wi = host_weights(L, weights["w_in"], weights["w_conv_out"], weights["w_out"], weights["w_gate_up"], weights["w_down"])
    prm, cm = host_params(L, weights["norm_mix"], weights["q_norm"], weights["k_norm"], weights["sinks"], weights["conv_w"],
                          weights["conv_b"], weights["conv_ln_g"], weights["conv_ln_b"], weights["norm_ffn"])
    in_maps = []
    nhalf = T // (P * R)
    for cid in core_ids:
        b, half = cid // nhalf, cid % nhalf
        m = make_core_inputs(x[b], half, P, NT)
        m.update(wi)
        m["params"] = prm
        m["cmat"] = cm
        in_maps.append(m)
    res = run_bass_kernel_spmd(nc, in_maps, core_ids=list(range(len(core_ids))))
    out = np.zeros((B, T, D), np.float32)
    for k, cid in enumerate(core_ids):
        b, half = cid // nhalf, cid % nhalf
        yT = res.results[k]["yT"]
        for p in range(P):
            start = half * P * R + p * R
            out[b, start:start + R] = yT[p].transpose(1, 0, 2).reshape(D, R).T
    return out


def kernel(x, norm_mix, w_in, q_norm, k_norm, sinks, conv_w, conv_b, conv_ln_g, conv_ln_b, w_conv_out, w_out,
           norm_ffn, w_gate_up, w_down):
    f = lambda a: np.ascontiguousarray(np.asarray(a, dtype=np.float32))
    weights = dict(norm_mix=f(norm_mix), w_in=f(w_in), q_norm=f(q_norm), k_norm=f(k_norm), sinks=f(sinks), conv_w=f(conv_w),
                   conv_b=f(conv_b), conv_ln_g=f(conv_ln_g), conv_ln_b=f(conv_ln_b), w_conv_out=f(w_conv_out), w_out=f(w_out),
                   norm_ffn=f(norm_ffn), w_gate_up=f(w_gate_up), w_down=f(w_down))
    return run(f(x), weights, P=2, NT=5, L=4, core_ids=list(range(8)))
```
